# Optimizing a Trainium2 kernel written in Bass

```python
import jax, jax.numpy as jnp
from jax import lax
import numpy as np

D_MODEL = 1024
BATCH = 16
SEQ = 2048
DEPTH = 1

CTX_LEN = 256
GRID_W = 64
MIX_WIDTH = D_MODEL
NA_HEAD_DIM = 64
NA_WIDTH = MIX_WIDTH // 2
NA_HEADS = NA_WIDTH // NA_HEAD_DIM
NA_WIN_ROWS = 8
NA_WIN_COLS = 16
RET_HEADS = 4
RET_WIDTH = MIX_WIDTH - NA_WIDTH
RET_V_DIM = RET_WIDTH // RET_HEADS
RET_QK_DIM = RET_WIDTH // RET_HEADS
RET_QK_WIDTH = RET_HEADS * RET_QK_DIM
RET_CHUNK = 128
IN_WIDTH = 4 * NA_WIDTH + 2 * RET_QK_WIDTH + 2 * RET_WIDTH
ROPE_BASE = 10000.0
EPS = 1e-6

kernel_name = "hybrid_na_retention_dit_block"


def _rmsnorm(x, g):
    xf = x.astype(jnp.float32)
    xf = xf * lax.rsqrt(jnp.mean(xf * xf, axis=-1, keepdims=True) + EPS)
    return (xf * g.astype(jnp.float32)).astype(x.dtype)


def _heads(t, n_heads):
    b, l, _ = t.shape
    return t.reshape(b, l, n_heads, -1).transpose(0, 2, 1, 3)


def _merge(t):
    b, h, l, d = t.shape
    return t.transpose(0, 2, 1, 3).reshape(b, l, h * d)


def _split_proj(p):
    sizes = [NA_WIDTH] * 4 + [RET_QK_WIDTH, RET_QK_WIDTH, RET_WIDTH, RET_WIDTH]
    offs = [int(o) for o in np.cumsum(sizes)[:-1]]
    return jnp.split(p, offs, axis=-1)


def _axial_rotary(x):
    L, dk = x.shape[2], x.shape[-1]
    half = dk // 2
    nf = half // 2
    t = jnp.arange(L)
    row = (t // GRID_W).astype(jnp.float32)
    col = (t % GRID_W).astype(jnp.float32)
    inv = ROPE_BASE ** (-jnp.arange(nf, dtype=jnp.float32) / nf)
    ang = jnp.concatenate([row[:, None] * inv, col[:, None] * inv], axis=-1)
    cos, sin = jnp.cos(ang), jnp.sin(ang)
    xf = x.astype(jnp.float32)
    x1, x2 = xf[..., :half], xf[..., half:]
    out = jnp.concatenate([x1 * cos - x2 * sin, x1 * sin + x2 * cos], axis=-1)
    return out.astype(x.dtype)


def _neighbourhood_attention(q, k, v, k_ctx, v_ctx, rpb):
    B, H, L, dh = q.shape
    rows = L // GRID_W
    kr = min(NA_WIN_ROWS, rows)
    kc = NA_WIN_COLS
    r = jnp.arange(rows)
    r0 = jnp.clip(r - kr // 2, 0, rows - kr)
    row_idx = r0[:, None] + jnp.arange(kr)[None, :]
    cq = jnp.arange(GRID_W)
    c0 = jnp.clip(cq - kc // 2, 0, GRID_W - kc)
    ck = jnp.arange(GRID_W)
    col_in = (ck[None, :] >= c0[:, None]) & (ck[None, :] < c0[:, None] + kc)
    dr_i = row_idx - r[:, None] + NA_WIN_ROWS - 1
    dc_i = jnp.clip(ck[None, :] - cq[:, None] + kc - 1, 0, 2 * kc - 2)
    bias = rpb[:, dr_i[:, None, :, None], dc_i[None, :, None, :]]
    bias = bias.reshape(H, rows, GRID_W, kr * GRID_W).astype(jnp.float32)
    mask = jnp.broadcast_to(col_in[:, None, :], (GRID_W, kr, GRID_W)).reshape(GRID_W, kr * GRID_W)

    kg = k.reshape(B, H, rows, GRID_W, dh)[:, :, row_idx].reshape(B, H, rows, kr * GRID_W, dh)
    vg = v.reshape(B, H, rows, GRID_W, dh)[:, :, row_idx].reshape(B, H, rows, kr * GRID_W, dh)
    qr = q.reshape(B, H, rows, GRID_W, dh)
    scale = dh ** -0.5
    s_loc = jnp.einsum('bhrqd,bhrkd->bhrqk', qr, kg, preferred_element_type=jnp.float32) * scale + bias[None]
    s_loc = jnp.where(mask, s_loc, -jnp.inf)
    s_ctx = jnp.einsum('bhrqd,bhcd->bhrqc', qr, k_ctx, preferred_element_type=jnp.float32) * scale
    p = jax.nn.softmax(jnp.concatenate([s_loc, s_ctx], axis=-1), axis=-1)
    p_loc = p[..., :kr * GRID_W].astype(v.dtype)
    p_ctx = p[..., kr * GRID_W:].astype(v.dtype)
    o = jnp.einsum('bhrqk,bhrkd->bhrqd', p_loc, vg) + jnp.einsum('bhrqc,bhcd->bhrqd', p_ctx, v_ctx)
    return o.reshape(B, H, L, dh)


def _dense_attention(q, k, v):
    s = jnp.einsum('bhqd,bhkd->bhqk', q, k, preferred_element_type=jnp.float32) * (q.shape[-1] ** -0.5)
    p = jax.nn.softmax(s, axis=-1).astype(v.dtype)
    return jnp.einsum('bhqk,bhkd->bhqd', p, v)


def _retention_chunkwise(q, k, v, log_gamma, state0):
    B, H, L, dk = q.shape
    dv = v.shape[-1]
    C = RET_CHUNK
    n = L // C
    qc = q.astype(jnp.float32).reshape(B, H, n, C, dk)
    kc = k.astype(jnp.float32).reshape(B, H, n, C, dk)
    vc = v.astype(jnp.float32).reshape(B, H, n, C, dv)
    lg = log_gamma.astype(jnp.float32)
    i = jnp.arange(C, dtype=jnp.float32)
    dist = i[:, None] - i[None, :]
    decay = jnp.where(dist >= 0, jnp.exp(lg[:, None, None] * jnp.maximum(dist, 0.0)), 0.0)
    s = jnp.einsum('bhnid,bhnjd->bhnij', qc, kc) * decay[None, :, None]
    inner = jnp.einsum('bhnij,bhnje->bhnie', s, vc)
    k_decay = jnp.exp(lg[:, None] * (C - 1 - i))
    k_w = kc * k_decay[None, :, None, :, None]
    chunk_states = jnp.einsum('bhnjd,bhnje->nbhde', k_w, vc)
    chunk_decay = jnp.exp(lg * C)[None, :, None, None]

    def step(S, T):
        return chunk_decay * S + T, S

    S_final, S_prev = lax.scan(step, state0, chunk_states)
    q_decay = jnp.exp(lg[:, None] * (i + 1.0))
    q_w = qc * q_decay[None, :, None, :, None]
    cross = jnp.einsum('bhnid,nbhde->bhnie', q_w, S_prev)
    return (inner + cross).reshape(B, H, L, dv), S_final


def _retention_final_state(k, v, log_gamma):
    L = k.shape[2]
    w = jnp.exp(log_gamma.astype(jnp.float32)[:, None] * (L - 1 - jnp.arange(L, dtype=jnp.float32)))
    return jnp.einsum('bhld,bhle->bhde', k.astype(jnp.float32) * w[None, :, :, None], v.astype(jnp.float32))


def _head_rmsnorm(o, g):
    o = o * lax.rsqrt(jnp.mean(o * o, axis=-1, keepdims=True) + EPS)
    return _merge(o) * g.astype(jnp.float32)


def _mixer(h_lat, h_ctx, w_in, na_rpb, lg_f, lg_b, ret_norm_g, w_out, with_ctx_out):
    dt = h_lat.dtype
    na_q, na_k, na_v, na_g, r_q, r_k, r_v, r_g = _split_proj(h_lat @ w_in)
    cna_q, cna_k, cna_v, cna_g, cr_q, cr_k, cr_v, cr_g = _split_proj(h_ctx @ w_in)

    k_na_ctx = _heads(cna_k, NA_HEADS)
    v_na_ctx = _heads(cna_v, NA_HEADS)
    o_na = _neighbourhood_attention(_heads(na_q, NA_HEADS), _heads(na_k, NA_HEADS),
                                    _heads(na_v, NA_HEADS), k_na_ctx, v_na_ctx, na_rpb)
    y_na = _merge(o_na) * jax.nn.silu(na_g)

    kscale = RET_QK_DIM ** -0.5
    q = _axial_rotary(_heads(r_q, RET_HEADS))
    k = _axial_rotary(_heads(r_k, RET_HEADS)) * kscale
    v = _heads(r_v, RET_HEADS)
    k_ctx = _heads(cr_k, RET_HEADS) * kscale
    v_ctx = _heads(cr_v, RET_HEADS)
    S_f = _retention_final_state(k_ctx, v_ctx, lg_f)
    S_b = _retention_final_state(jnp.flip(k_ctx, 2), jnp.flip(v_ctx, 2), lg_b)
    o_f, _ = _retention_chunkwise(q, k, v, lg_f, S_f)
    o_b, _ = _retention_chunkwise(jnp.flip(q, 2), jnp.flip(k, 2), jnp.flip(v, 2), lg_b, S_b)
    o_ret = o_f + jnp.flip(o_b, 2)
    y_ret = _head_rmsnorm(o_ret, ret_norm_g).astype(dt) * jax.nn.silu(r_g)

    y_lat = jnp.concatenate([y_na, y_ret], axis=-1) @ w_out
    if not with_ctx_out:
        return y_lat, None

    o_cna = _dense_attention(_heads(cna_q, NA_HEADS), k_na_ctx, v_na_ctx)
    yc_na = _merge(o_cna) * jax.nn.silu(cna_g)
    q_ctx = _heads(cr_q, RET_HEADS)
    zeros = jnp.zeros(S_f.shape, jnp.float32)
    oc_f, _ = _retention_chunkwise(q_ctx, k_ctx, v_ctx, lg_f, zeros)
    oc_b, _ = _retention_chunkwise(jnp.flip(q_ctx, 2), jnp.flip(k_ctx, 2), jnp.flip(v_ctx, 2), lg_b, zeros)
    yc_ret = _head_rmsnorm(oc_f + jnp.flip(oc_b, 2), ret_norm_g).astype(dt) * jax.nn.silu(cr_g)
    y_ctx = jnp.concatenate([yc_na, yc_ret], axis=-1) @ w_out
    return y_lat, y_ctx


def setup_inputs(seed: int = 0) -> dict:
    key = jax.random.key(seed)
    ks = jax.random.split(key, 16)
    f32 = jnp.float32
    D = D_MODEL
    x = jax.random.normal(ks[0], (BATCH, SEQ, D), f32)
    c = jax.random.normal(ks[1], (BATCH, D), f32)
    ctx = jax.random.normal(ks[2], (BATCH, CTX_LEN, D), f32)
    c_ctx = jax.random.normal(ks[3], (D,), f32)
    norm_g = 1.0 + 0.02 * jax.random.normal(ks[4], (DEPTH, D), f32)
    w_ada = 0.5 * D ** -0.5 * jax.random.normal(ks[5], (DEPTH, D, 3 * D), f32)
    b_ada = 0.01 * jax.random.normal(ks[6], (DEPTH, 3 * D), f32)
    w_in = D ** -0.5 * jax.random.normal(ks[7], (DEPTH, D, IN_WIDTH), f32)
    na_rpb = 0.1 * jax.random.normal(ks[8], (DEPTH, NA_HEADS, 2 * NA_WIN_ROWS - 1, 2 * NA_WIN_COLS - 1), f32)
    gam = 1.0 - 2.0 ** (-5.0 - np.arange(RET_HEADS, dtype=np.float32))
    base = jnp.asarray(np.log(-np.log(gam)).astype(np.float32))
    ret_decay_fwd = base[None] + 0.05 * jax.random.normal(ks[9], (DEPTH, RET_HEADS), f32)
    ret_decay_bwd = base[None] + 0.05 * jax.random.normal(ks[10], (DEPTH, RET_HEADS), f32)
    ret_norm_g = 1.0 + 0.02 * jax.random.normal(ks[11], (DEPTH, RET_WIDTH), f32)
    w_out = MIX_WIDTH ** -0.5 * jax.random.normal(ks[12], (DEPTH, MIX_WIDTH, D), f32)
    final_norm_g = 1.0 + 0.02 * jax.random.normal(ks[13], (D,), f32)
    return {"x": x, "c": c, "ctx": ctx, "c_ctx": c_ctx, "norm_g": norm_g, "w_ada": w_ada,
            "b_ada": b_ada, "w_in": w_in, "na_rpb": na_rpb, "ret_decay_fwd": ret_decay_fwd,
            "ret_decay_bwd": ret_decay_bwd, "ret_norm_g": ret_norm_g, "w_out": w_out,
            "final_norm_g": final_norm_g}


def reference(x, c, ctx, c_ctx, norm_g, w_ada, b_ada, w_in, na_rpb, ret_decay_fwd,
              ret_decay_bwd, ret_norm_g, w_out, final_norm_g):
    D = D_MODEL
    for i in range(DEPTH):
        update_ctx = i < DEPTH - 1
        mod = jax.nn.silu(c) @ w_ada[i] + b_ada[i]
        mod_c = jax.nn.silu(c_ctx) @ w_ada[i] + b_ada[i]
        shift, scale, gate = mod[:, None, :D], mod[:, None, D:2 * D], mod[:, None, 2 * D:]
        shift_c, scale_c, gate_c = mod_c[:D], mod_c[D:2 * D], mod_c[2 * D:]
        h_lat = _rmsnorm(x, norm_g[i]) * (1.0 + scale) + shift
        h_ctx = _rmsnorm(ctx, norm_g[i]) * (1.0 + scale_c) + shift_c
        lg_f = -jnp.exp(ret_decay_fwd[i].astype(jnp.float32))
        lg_b = -jnp.exp(ret_decay_bwd[i].astype(jnp.float32))
        y_lat, y_ctx = _mixer(h_lat, h_ctx, w_in[i], na_rpb[i], lg_f, lg_b, ret_norm_g[i],
                              w_out[i], update_ctx)
        x = x + gate * y_lat
        if update_ctx:
            ctx = ctx + gate_c * y_ctx
    return _rmsnorm(x, final_norm_g)
```

```python
from contextlib import ExitStack
import numpy as np
import concourse.bass as bass
import concourse.mybir as mybir
from concourse.bass_utils import run_bass_kernel_spmd

F32 = mybir.dt.float32
BF16 = mybir.dt.bfloat16
AF = mybir.ActivationFunctionType
ALU = mybir.AluOpType
AX = mybir.AxisListType

D = 1024
L = 2048
LC = 256
LT = L + LC
NT = LT // 128
GW = 64
EPS = 1e-6
NB = 2
CH = 128
INTERLEAVE_NA = True
FIN_DELAY = 7
RET_SCHED = 1
NSC = 4
RET_SILU = True
ATTACH_WAITS = True
TRANSITIVE_WAITS = True


class Buf:
    __slots__ = ("name", "w", "r", "dsem", "dcnt", "excl")

    def __init__(self, name, excl=False):
        self.name = name
        self.excl = excl
        self.w = None
        self.r = {}
        self.dsem = None
        self.dcnt = 0


class TK:
    ENGS = ("tensor", "vector", "scalar", "gpsimd", "sync")

    def __init__(self, nc, stack):
        self.nc = nc
        self.stack = stack
        self.esem = {e: stack.enter_context(nc.semaphore("es_" + e)) for e in self.ENGS}
        self.ecnt = {e: 0 for e in self.ENGS}
        self.seen = {e: {} for e in self.ENGS}
        self.prog = {e: [] for e in self.ENGS}
        self.nsem = 0
        self.clock = {}
        self.ev_seq = {}
        self.seq = 0
        self.same_engine_sync = True
        self.same_engine_war = True

    def _collect(self, eng, reads, writes):
        need = {}

        def add(ev, raw=True):
            if ev is None:
                return
            sem, val, src = ev
            if src == eng and (eng == "tensor" or not self.same_engine_sync or (not raw and not self.same_engine_war)):
                return
            if need.get(sem, 0) < val:
                need[sem] = val

        for b in reads:
            add(b.w)
            if b.excl:
                for ev in b.r.values():
                    if ev[2] != eng:
                        add(ev)
        for b in writes:
            add(b.w, raw=False)
            for ev in b.r.values():
                add(ev, raw=False)
        seen = self.seen[eng]
        order = sorted(need.items(), key=lambda kv: -self.ev_seq.get((kv[0], kv[1]), 0))
        for sem, val in order:
            if seen.get(sem, 0) >= val:
                continue
            seen[sem] = val
            self.prog[eng].append(("wait", sem, val))
            clk = self.clock.get((sem, val))
            if clk is not None and TRANSITIVE_WAITS:
                for s2, v2 in clk.items():
                    if seen.get(s2, 0) < v2:
                        seen[s2] = v2

    def _record(self, ev, reads, writes):
        sem = ev[0]
        for b in reads:
            old = b.r.get(sem)
            if old is None or old[1] < ev[1]:
                b.r[sem] = ev
        for b in writes:
            b.w = ev
            b.r = {}

    def op(self, eng, fn, reads=(), writes=(), inc=True):
        self._collect(eng, reads, writes)
        self.seq += 1
        if inc:
            self.ecnt[eng] += 1
            ev = (self.esem[eng], self.ecnt[eng], eng)
            clk = dict(self.seen[eng])
            if eng == "tensor" or self.ecnt[eng] > 1:
                pass
            self.clock[(ev[0], ev[1])] = clk
        else:
            ev = (self.esem[eng], self.ecnt[eng] + 1, eng)
        self.ev_seq[(ev[0], ev[1])] = self.seq
        self.prog[eng].append(("op", fn, inc))
        self._record(ev, reads, writes)

    def dma(self, eng, out, in_, reads=(), writes=(), **kw):
        b = writes[0] if writes else reads[0]
        if b.dsem is None:
            b.dsem = self.stack.enter_context(self.nc.semaphore("ds%d" % self.nsem))
            self.nsem += 1
        self._collect(eng, reads, writes)
        if b.dcnt > 0 and self.seen[eng].get(b.dsem, 0) < b.dcnt:
            self.seen[eng][b.dsem] = b.dcnt
            self.prog[eng].append(("wait", b.dsem, b.dcnt))
        b.dcnt += 16
        ev = (b.dsem, b.dcnt, None)
        self.seq += 1
        self.clock[(ev[0], ev[1])] = dict(self.seen[eng])
        self.ev_seq[(ev[0], ev[1])] = self.seq
        self.prog[eng].append(("dma", out, in_, b.dsem, kw))
        self._record(ev, reads, writes)
        return ev

    def barrier(self, bufs):
        for e in self.ENGS:
            for e2 in self.ENGS:
                if (e2 != e or e != "tensor") and self.ecnt[e2] > self.seen[e].get(self.esem[e2], 0):
                    self.seen[e][self.esem[e2]] = self.ecnt[e2]
                    self.prog[e].append(("wait", self.esem[e2], self.ecnt[e2]))
            for b in bufs:
                if b.dsem is not None and b.dcnt > self.seen[e].get(b.dsem, 0):
                    self.seen[e][b.dsem] = b.dcnt
                    self.prog[e].append(("wait", b.dsem, b.dcnt))

    def wait_ev(self, eng, ev):
        sem, val, _ = ev
        if self.seen[eng].get(sem, 0) < val:
            self.seen[eng][sem] = val
            self.prog[eng].append(("wait", sem, val))

    def replay(self, eng, e):
        sem = self.esem[eng]
        pend = None
        for it in self.prog[eng]:
            if it[0] == "wait":
                if pend is not None:
                    e.wait_ge(pend[1], pend[2])
                pend = it if ATTACH_WAITS else None
                if not ATTACH_WAITS:
                    e.wait_ge(it[1], it[2])
            elif it[0] == "op":
                ins = it[1](e)
                if pend is not None:
                    ins._wait_ge(pend[1], pend[2])
                    pend = None
                if it[2]:
                    ins.then_inc(sem, 1)
            else:
                if pend is not None:
                    e.wait_ge(pend[1], pend[2])
                    pend = None
                e.dma_start(out=it[1], in_=it[2], **it[4]).then_inc(it[3], 16)
        if pend is not None:
            e.wait_ge(pend[1], pend[2])

    def emit(self):
        nc = self.nc
        with nc.Block() as block:
            @block.tensor
            def _(e):
                self.replay("tensor", e)

            @block.vector
            def _(e):
                self.replay("vector", e)

            @block.scalar
            def _(e):
                self.replay("scalar", e)

            @block.gpsimd
            def _(e):
                self.replay("gpsimd", e)

            @block.sync
            def _(e):
                self.replay("sync", e)


KSCALE = float(128 ** -0.5)
NSLOT = 16


def _consts():
    f = np.float32
    c = {}
    c["ident"] = np.eye(128, dtype=f)
    tt = np.arange(L)
    row = (tt // GW).astype(np.float64)
    col = (tt % GW).astype(np.float64)
    nf = 32
    inv = 10000.0 ** (-np.arange(nf, dtype=np.float64) / nf)
    ang = np.concatenate([row[:, None] * inv, col[:, None] * inv], axis=-1)
    cs = np.stack([np.cos(ang).astype(f), np.sin(ang).astype(f)], axis=0)
    c["cossin"] = np.ascontiguousarray(cs.reshape(2, 16, 128, 64).transpose(2, 0, 1, 3).reshape(128, 2 * 16 * 64))
    p = np.arange(128)
    kp, kc = p // 64, p % 64
    qc = np.arange(64)
    c0 = np.clip(qc - 8, 0, 48)
    colv = (kc[:, None] >= c0[None, :]) & (kc[:, None] < c0[None, :] + 16)
    sl = np.arange(NSLOT)
    delta = 7 + kp[:, None] - sl[None, :]
    okf = (delta >= -7) & (delta <= 7) & (sl[None, :] >= 1) & (sl[None, :] <= 14)
    oki = okf & (delta >= -4) & (delta <= 3)
    c["cm_full"] = np.ascontiguousarray((okf[:, :, None] & colv[:, None, :]).astype(f).reshape(128, NSLOT * 64))
    c["cm_int"] = np.ascontiguousarray((oki[:, :, None] & colv[:, None, :]).astype(f).reshape(128, NSLOT * 64))
    s_ = np.arange(128)[:, None].astype(f)
    t_ = np.arange(128)[None, :].astype(f)
    dposF = np.maximum(t_ - s_, 0)
    dposB = np.maximum(s_ - t_, 0)
    maskF = (t_ > s_).astype(f) * f(KSCALE)
    maskB = (s_ > t_).astype(f) * f(KSCALE)
    eye2 = np.eye(128, dtype=f) * f(2.0 * KSCALE)
    rt1 = np.broadcast_to(t_ + 1, (128, 128)).astype(f)
    rt2 = np.broadcast_to(128 - t_, (128, 128)).astype(f)
    cols = np.stack([127 - s_[:, 0], s_[:, 0], 255 - s_[:, 0], 128 + s_[:, 0]], axis=1).astype(f)
    c["rconst"] = np.ascontiguousarray(np.concatenate([dposF, dposB, maskF, maskB, eye2, rt1, rt2, cols], axis=1))
    oeo = np.zeros((128, 128), f)
    oeo[64, 0:64] = 1.0
    oeo[0, 64:128] = 1.0
    c["ones_eo"] = oeo
    return c


def _bias_table(rpb):
    p = np.arange(128)
    kp, kc = p // 64, p % 64
    sl = np.arange(NSLOT)
    qc = np.arange(64)
    ri = np.clip(14 + kp[:, None] - sl[None, :], 0, 14)
    ci = np.clip(kc[:, None] - qc[None, :] + 15, 0, 30)
    tab = rpb[:, ri[:, :, None], ci[:, None, :]]
    return np.ascontiguousarray(tab.transpose(1, 0, 2, 3).reshape(128, 8 * NSLOT * 64).astype(np.float32))


def build_nc(nb=NB, debug=None, stop=None):
    nc = bass.Bass("TRN2", target_bir_lowering=False)
    dbg = debug or ()

    def din(name, shape):
        return nc.dram_tensor(name, list(shape), F32, kind="ExternalInput").ap()

    x_d = din("x", [nb, L, D])
    ctx_d = din("ctx", [nb, LC, D])
    c_d = din("c3", [3, D])
    normg_d = din("norm_g", [1, D])
    wada_d = din("w_ada", [D, 3 * D])
    bada_d = din("b_ada", [1, 3 * D])
    win_d = din("w_in", [D, 4096])
    wout_d = din("w_out", [D, D])
    fng_d = din("final_norm_g", [1, D])
    rngc_d = din("rng_col", [128, 4])
    decay_d = din("decay8", [1, 8])
    btab_d = din("bias_tab", [128, 8 * NSLOT * 64])
    ident_d = din("ident", [128, 128])
    cossin_d = din("cossin", [128, 2 * 16 * 64])
    cmf_d = din("cm_full", [128, NSLOT * 64])
    cmi_d = din("cm_int", [128, NSLOT * 64])
    rconst_d = din("rconst", [128, 7 * 128 + 4])
    oeo_d = din("ones_eo", [128, 128])
    out_d = nc.dram_tensor("out", [nb, L, D], F32, kind="ExternalOutput").ap()

    dbg_out = {}
    if "hT" in dbg:
        dbg_out["hT"] = nc.dram_tensor("dbg_hT", [128, 8 * LT], F32, kind="ExternalOutput").ap()
    if "yT" in dbg:
        dbg_out["yT"] = nc.dram_tensor("dbg_yT", [128, 8 * L], F32, kind="ExternalOutput").ap()
    if "wo" in dbg:
        dbg_out["wo"] = nc.dram_tensor("dbg_wo", [128, 8 * D], F32, kind="ExternalOutput").ap()
    if "mod" in dbg:
        dbg_out["mod"] = nc.dram_tensor("dbg_mod", [3, 3 * D], F32, kind="ExternalOutput").ap()

    mod_scr = nc.dram_tensor("mod_scr", [3, 3 * D], F32).ap()

    V, S_, P_, T_, Y_ = "vector", "scalar", "gpsimd", "tensor", "sync"

    with ExitStack() as st:
        tk = TK(nc, st)
        B = {}

        def buf(name, excl=False):
            if name not in B:
                B[name] = Buf(name, excl)
            return B[name]

        uniq = [0]

        def sbt(stack, name, shape, dt):
            uniq[0] += 1
            return stack.enter_context(nc.sbuf_tensor("sb%d_%s" % (uniq[0], name), list(shape), dt))

        def sb(name, shape, dt):
            return sbt(st, name, shape, dt)

        def barrier():
            tk.barrier(list(B.values()))

        def psb(name, shape, dt):
            t = st.enter_context(nc.psum_tensor("ps_" + name, list(shape), dt))
            return t, buf("ps_" + name, excl=True)

        pbank = [psb("b%d" % i, [128, 512], F32) for i in range(8)]
        acc = pbank[0:2]
        sc = pbank[2:4]
        ob = pbank[4:6]
        bcb, bcB = pbank[6]
        tpB = pbank[7][1]
        tpb = pbank[7][0][:, :].bitcast(BF16).rearrange("p (a b) -> p a b", a=8)
        tpb2 = pbank[6][0][:, :].bitcast(BF16).rearrange("p (a b) -> p a b", a=8)
        tp2 = [(tpb, tpB), (tpb2, bcB)]

        ident = sb("ident", [128, 128], BF16)
        hT = sb("hT", [128, 8, LT], BF16)
        yT = sb("yT", [128, 8, L], BF16)
        ss = sb("ss", [128, 16], F32)
        cneg = sb("cneg", [128, 4], F32)
        Mtab = [sb("Mfull", [128, 8, NSLOT, 64], BF16), sb("Mint", [128, 8, NSLOT, 64], BF16)]
        cossin = sb("cossin", [128, 2, 16, 64], F32)
        oeo = sb("oeo", [128, 128], F32)
        lg = sb("lg", [128, 8], F32)
        DT = sb("DT", [128, 4, 128], F32)
        tabq = sb("tabq", [128, 4, 2, 128], F32)
        rcols = sb("rcols", [128, 4, 8], F32)
        rngc = sb("rngc", [128, 4], F32)

        wbuf = [sb("wbuf%d" % i, [128, 8, 512], BF16) for i in range(2)]
        wbufB = [[buf("wbuf%d_s%d" % (i, j)) for j in range(4)] for i in range(2)]
        wctr = [0]
        tk.op(P_, lambda e: e.memset(cneg[:, :], -0.5), writes=[buf("cneg")])

        s0 = ExitStack()
        if True:
            stg = sbt(s0, "stg", [128, 256], F32)
            stg2 = sbt(s0, "stg2", [128, 1024], F32)
            cmf = sbt(s0, "cmf", [128, NSLOT * 64], F32)
            cmi = sbt(s0, "cmi", [128, NSLOT * 64], F32)
            rconst = sbt(s0, "rconst", [128, 7 * 128 + 4], F32)
            c3T = sbt(s0, "c3T", [128, 3, 8], F32)
            c3s = sbt(s0, "c3s", [128, 8, 4], BF16)
            mod3 = [sbt(s0, "mod3_%d" % i, [3, 256], F32) for i in range(2)]
            bada3 = [sbt(s0, "bada3_%d" % i, [3, 256], F32) for i in range(2)]
            dec8 = sbt(s0, "dec8", [128, 8], F32)

            for r in range(3):
                tk.dma(Y_, c3T[:, r, :], c_d[r, :].rearrange("(k p) -> p k", p=128), writes=[buf("c3T%d" % r)],
                       allow_slow_non_contiguous=True)
            tk.dma(Y_, stg[:, 0:128], ident_d[:, :], writes=[buf("stg")])
            tk.op(V, lambda e: e.tensor_copy(out=ident[:, :], in_=stg[:, 0:128]),
                  reads=[buf("stg")], writes=[buf("ident")])
            tk.dma(Y_, cossin[:, :, :, :].rearrange("p a t d -> p (a t d)"), cossin_d[:, :], writes=[buf("cossin")])
            tk.dma(Y_, oeo[:, :], oeo_d[:, :], writes=[buf("oeo")])
            tk.dma(Y_, rngc[:, :], rngc_d[:, :], writes=[buf("rngc")])
            tk.dma(Y_, cmf[:, :], cmf_d[:, :], writes=[buf("cmf")])
            tk.dma(Y_, cmi[:, :], cmi_d[:, :], writes=[buf("cmi")])
            tk.dma(Y_, rconst[:, :], rconst_d[:, :], writes=[buf("rconst")])
            tk.dma(Y_, dec8[:, :], decay_d[0:1, :].to_broadcast([128, 8]), writes=[buf("dec8")])

            c3v = c3T[:, :, :].rearrange("p r k -> p (r k)")
            c3sv = c3s[:, :, 0:3].rearrange("p k r -> p r k")
            c3B = [buf("c3T%d" % r) for r in range(3)]
            tk.op(S_, lambda e: e.activation(out=stg2[:, 0:24], in_=c3v, func=AF.Tanh, scale=0.5),
                  reads=c3B, writes=[buf("stg2")])
            tk.op(V, lambda e: e.scalar_tensor_tensor(out=stg2[:, 32:56], in0=stg2[:, 0:24], scalar=1.0, in1=c3v,
                                                      op0=ALU.add, op1=ALU.mult),
                  reads=[buf("stg2")] + c3B, writes=[buf("stg2b")])
            tk.op(V, lambda e: e.tensor_scalar(out=c3sv, in0=stg2[:, 32:56].rearrange("p (r k) -> p r k", r=3),
                                               scalar1=0.5, scalar2=None, op0=ALU.mult),
                  reads=[buf("stg2b")], writes=[buf("c3s")])
            for J in range(6):
                wb, wbBl = wbuf[J % 2], wbufB[J % 2]
                pm, pmB = pbank[6 + (J % 2)]
                tk.dma(P_, wb[:, :, :], wada_d[:, J * 512:(J + 1) * 512].rearrange("(k p) n -> p k n", p=128),
                       writes=wbBl)
                for k in range(8):
                    tk.op(T_, lambda e, k=k, wb=wb, pm=pm: e.matmul(pm[0:3, 0:512], lhsT=c3s[:, k, 0:3], rhs=wb[:, k, :],
                                                                     start=(k == 0), stop=(k == 7)),
                          reads=[buf("c3s")] + wbBl, writes=[pmB], inc=(k == 7))
                for half in range(2):
                    j = 2 * J + half
                    m3, m3B = mod3[j % 2], buf("mod3_%d" % (j % 2))
                    b3, b3B = bada3[j % 2], buf("bada3_%d" % (j % 2))
                    tk.dma(Y_, b3[:, :], bada_d[0:1, j * 256:(j + 1) * 256].to_broadcast([3, 256]), writes=[b3B])
                    tk.op(V, lambda e, pm=pm, m3=m3, b3=b3, half=half: e.tensor_tensor(
                        out=m3[:, :], in0=pm[0:3, half * 256:(half + 1) * 256], in1=b3[:, :], op=ALU.add),
                        reads=[pmB, b3B], writes=[m3B])
                    tk.dma(Y_, mod_scr[:, j * 256:(j + 1) * 256], m3[:, :], reads=[m3B],
                           writes=[buf("mod_scr" if j < 8 else "mod_scr_g")])

        def setup_part2(stA, stAB, stE, stEB):
            tk.op(S_, lambda e: e.activation(out=dec8[:, :], in_=dec8[:, :], func=AF.Exp),
                  reads=[buf("dec8")], writes=[buf("dec8")])
            tk.op(V, lambda e: e.tensor_scalar(out=lg[:, :], in0=dec8[:, :], scalar1=-1.0, scalar2=None, op0=ALU.mult),
                  reads=[buf("dec8")], writes=[buf("lg")])
            RC = lambda i: rconst[:, i * 128:(i + 1) * 128]
            lnk = float(np.log(KSCALE))

            def ret_consts(h):
                lf, lb = lg[:, h:h + 1], lg[:, 4 + h:5 + h]
                sA = stA[:, (h % 2) * 256:(h % 2) * 256 + 128]
                sB2 = stA[:, (h % 2) * 256 + 128:(h % 2) * 256 + 256]
                hB = buf("stA_h%d" % (h % 2))
                deps = [stAB, hB]
                tk.op(S_, lambda e: e.activation(out=sA, in_=RC(0), func=AF.Exp, scale=lf),
                      reads=[buf("rconst"), buf("lg")], writes=deps)
                tk.op(S_, lambda e: e.activation(out=sB2, in_=RC(1), func=AF.Exp, scale=lb),
                      reads=[buf("rconst"), buf("lg")], writes=[hB])
                tk.op(V, lambda e: e.tensor_tensor(out=sA, in0=sA, in1=RC(2), op=ALU.mult),
                      reads=[hB, buf("rconst")], writes=[hB])
                tk.op(V, lambda e: e.tensor_tensor(out=sB2, in0=sB2, in1=RC(3), op=ALU.mult),
                      reads=[hB, buf("rconst")], writes=[hB])
                tk.op(V, lambda e: e.tensor_tensor(out=sA, in0=sA, in1=sB2, op=ALU.add),
                      reads=[hB], writes=[hB])
                tk.op(V, lambda e: e.tensor_tensor(out=DT[:, h, :], in0=sA, in1=RC(4), op=ALU.add),
                      reads=[hB, buf("rconst")], writes=[buf("DT"), stAB])
                tk.op(S_, lambda e: e.activation(out=tabq[:, h, 0, :], in_=RC(5), func=AF.Exp, scale=lf, bias=lnk),
                      reads=[buf("rconst"), buf("lg")], writes=[buf("tabq")])
                tk.op(S_, lambda e: e.activation(out=tabq[:, h, 1, :], in_=RC(6), func=AF.Exp, scale=lb, bias=lnk),
                      reads=[buf("rconst"), buf("lg")], writes=[buf("tabq")])
                cc = lambda i: rconst[:, 7 * 128 + i:7 * 128 + i + 1]
                for j, (src, sc_) in enumerate([(cc(0), lf), (cc(1), lb), (cc(2), lf), (cc(3), lb),
                                                (rconst[:, 6 * 128:6 * 128 + 1], lf),
                                                (rconst[:, 6 * 128:6 * 128 + 1], lb)]):
                    tk.op(S_, lambda e, j=j, src=src, sc_=sc_: e.activation(out=rcols[:, h, j:j + 1], in_=src,
                                                                            func=AF.Exp, scale=sc_),
                          reads=[buf("rconst"), buf("lg")], writes=[buf("rcols")])

            def na_table(h):
                sE, sEB = stE[h % 2], stEB[h % 2]
                tk.dma(Y_, sE[:, :], btab_d[:, h * 1024:(h + 1) * 1024], writes=[sEB])
                tk.op(S_, lambda e: e.activation(out=sE[:, :], in_=sE[:, :], func=AF.Exp),
                      reads=[sEB], writes=[sEB])
                tk.op(V, lambda e: e.tensor_tensor(out=Mtab[0][:, h, :, :].rearrange("p s q -> p (s q)"),
                                                   in0=sE[:, :], in1=cmf[:, :], op=ALU.mult),
                      reads=[sEB, buf("cmf")], writes=[buf("Mtab")])
                tk.op(P_, lambda e: e.tensor_tensor(out=Mtab[1][:, h, :, :].rearrange("p s q -> p (s q)"),
                                                    in0=sE[:, :], in1=cmi[:, :], op=ALU.mult),
                      reads=[sEB, buf("cmi")], writes=[buf("Mtab")])

            units = []
            for h in range(8):
                units.append(lambda h=h: na_table(h))
                if h < 4:
                    units.append(lambda h=h: ret_consts(h))
            return units

        if stop == "setup":
            nb = 0


        def next_w():
            i = wctr[0] % 2
            wctr[0] += 1
            return wbuf[i], wbufB[i]

        def do_batch(b):
            wgv = next_w()
            with ExitStack() as sA:
                xs = [sbt(sA, "xsA%d" % i, [128, D], F32) for i in range(3)]
                junk = sbt(sA, "junkA", [128, D], BF16)
                t32s = [sbt(sA, "t32A%d" % i, [128, D], F32) for i in range(2)]
                hbs = [sbt(sA, "hbA%d" % i, [128, D], BF16) for i in range(2)]
                A_t = sbt(sA, "A_t", [128, D], F32)
                sh_t = sbt(sA, "sh_t", [128, D], F32)
                A_c = sbt(sA, "A_c", [128, D], F32)
                sh_c = sbt(sA, "sh_c", [128, D], F32)
                g_t = sbt(sA, "g_t", [128, D], F32)
                ssA = sbt(sA, "ssA", [128, 4, 4], F32)
                tk.dma(Y_, g_t[:, :], normg_d[0:1, :].to_broadcast([128, D]), writes=[buf("g_t")])

                def prep_mod(row, At, Sh, nm):
                    tk.dma(Y_, At[:, :], mod_scr[row:row + 1, D:2 * D].to_broadcast([128, D]),
                           reads=[buf("mod_scr")], writes=[buf("A_" + nm)])
                    tk.dma(Y_, Sh[:, :], mod_scr[row:row + 1, 0:D].to_broadcast([128, D]),
                           reads=[buf("mod_scr")], writes=[buf("sh_" + nm)])
                    tk.op(V, lambda e: e.scalar_tensor_tensor(out=At[:, :], in0=At[:, :], scalar=1.0, in1=g_t[:, :],
                                                              op0=ALU.add, op1=ALU.mult),
                          reads=[buf("A_" + nm), buf("g_t")], writes=[buf("A_" + nm)])


                def A_s0(t):
                    xsb, xB = xs[t % 3], buf("xsA%d" % (t % 3))
                    src = x_d[b, t * 128:(t + 1) * 128, :] if t < 16 else ctx_d[b, (t - 16) * 128:(t - 15) * 128, :]
                    tk.dma(Y_, xsb[:, :], src, writes=[xB])

                A_s0(0)
                A_s0(1)
                prep_mod(b, A_t, sh_t, "t")
                p2_units = []
                if b == 0:
                    scr0 = sbt(sA, "scrA0", [128, 512], F32)
                    p2_units = setup_part2(scr0, buf("scrA0"), [A_c, sh_c], [buf("A_c"), buf("sh_c")])

                def A_s1(t):
                    xsb, xB = xs[t % 3], buf("xsA%d" % (t % 3))
                    c = t % 4
                    sB_ = buf("ssA%d" % c)
                    tk.op(S_, lambda e: e.activation(out=junk[:, :], in_=xsb[:, :], func=AF.Square,
                                                     accum_out=ssA[:, c, 0:1]),
                          reads=[xB], writes=[buf("junk"), sB_])
                    tk.op(V, lambda e: e.tensor_scalar(out=ssA[:, c, 1:2], in0=ssA[:, c, 0:1], scalar1=1.0 / D, scalar2=EPS,
                                                       op0=ALU.mult, op1=ALU.add),
                          reads=[sB_], writes=[sB_])
                    tk.op(P_, lambda e: e.tensor_tensor(out=ssA[:, c, 2:3], in0=ssA[:, c, 1:2], in1=cneg[:, 0:1], op=ALU.pow),
                          reads=[sB_, buf("cneg")], writes=[sB_])

                def A_s2(t):
                    xsb, xB = xs[t % 3], buf("xsA%d" % (t % 3))
                    c = t % 4
                    t32, t32B = t32s[t % 2], buf("t32A%d" % (t % 2))
                    hb, hbB = hbs[t % 2], buf("hbA%d" % (t % 2))
                    At, Sh, nm = (A_t, sh_t, "t") if t < 16 else (A_c, sh_c, "c")
                    tk.op(V, lambda e: e.scalar_tensor_tensor(out=t32[:, :], in0=xsb[:, :], scalar=ssA[:, c, 2:3],
                                                              in1=At[:, :], op0=ALU.mult, op1=ALU.mult),
                          reads=[xB, buf("ssA%d" % c), buf("A_" + nm)], writes=[t32B])

                def A_s2b(t):
                    t32, t32B = t32s[t % 2], buf("t32A%d" % (t % 2))
                    hb, hbB = hbs[t % 2], buf("hbA%d" % (t % 2))
                    At, Sh, nm = (A_t, sh_t, "t") if t < 16 else (A_c, sh_c, "c")
                    tk.op(P_, lambda e: e.tensor_tensor(out=hb[:, 0:512], in0=t32[:, 0:512], in1=Sh[:, 0:512], op=ALU.add),
                          reads=[t32B, buf("sh_" + nm)], writes=[hbB])
                    tk.op(V, lambda e: e.tensor_tensor(out=hb[:, 512:1024], in0=t32[:, 512:1024], in1=Sh[:, 512:1024],
                                                       op=ALU.add),
                          reads=[t32B, buf("sh_" + nm)], writes=[buf("hbA%dh" % (t % 2))])

                def A_s3(t):
                    hb, hbB = hbs[t % 2], buf("hbA%d" % (t % 2))
                    tp_, tpB_ = tp2[t % 2]
                    for k in range(8):
                        tk.op(T_, lambda e, k=k: e.transpose(out=tp_[:, k, :], in_=hb[:, k * 128:(k + 1) * 128],
                                                             identity=ident[:, :]),
                              reads=[hbB, buf("hbA%dh" % (t % 2)), buf("ident")], writes=[tpB_], inc=(k == 7))
                    tk.op(S_, lambda e: e.copy(out=hT[:, :, t * 128:(t + 1) * 128], in_=tp_[:, :, :]),
                          reads=[tpB_], writes=[buf("hT%d" % t)])

                for step in range(NT + 4):
                    if step == 6:
                        tk.dma(P_, wgv[0][:, :, :], win_d[:, 1024:1536].rearrange("(k p) n -> p k n", p=128),
                               writes=wgv[1])
                    if step == 15:
                        assert not p2_units
                        prep_mod(2, A_c, sh_c, "c")
                    if step >= 2 and p2_units:
                        p2_units.pop(0)()
                    if 2 <= step < NT:
                        A_s0(step)
                    if 0 <= step - 4 < NT:
                        A_s3(step - 4)
                    if 0 <= step - 3 < NT:
                        A_s2b(step - 3)
                    if 0 <= step - 2 < NT:
                        A_s2(step - 2)
                    if 0 <= step - 1 < NT:
                        A_s1(step - 1)
                for u in p2_units:
                    u()
                barrier()
            if b == 0:
                s0.close()
            hT_all = [buf("hT%d" % t) for t in range(NT)]

            if "hT" in dbg and b == 0:
                with ExitStack() as sd:
                    dbg32 = sbt(sd, "dbg32", [128, 1152], F32)
                    for k in range(8):
                        for hh in range(0, LT, 1152):
                            tk.op(V, lambda e, k=k, hh=hh: e.tensor_copy(out=dbg32[:, :], in_=hT[:, k, hh:hh + 1152]),
                                  reads=hT_all, writes=[buf("dbg32")])
                            tk.dma(Y_, dbg_out["hT"][:, k * LT + hh:k * LT + hh + 1152], dbg32[:, :],
                                   reads=[buf("dbg32")], writes=[buf("dbg_hT")])
                    barrier()
            if stop == "A":
                return

            with ExitStack() as sB:

                evac_ctr = [0]

                def inproj_tok(t, wb, wbB, ncol, c0=0):
                    a, aB = acc[evac_ctr[0] % 2]
                    evac_ctr[0] += 1
                    for k in range(8):
                        tk.op(T_, lambda e, k=k, a=a: e.matmul(a[:, 0:ncol], lhsT=hT[:, k, t * 128:(t + 1) * 128],
                                                               rhs=wb[:, k, c0:c0 + ncol], start=(k == 0), stop=(k == 7)),
                              reads=[hT_all[t]] + wbB, writes=[aB], inc=(k == 7))
                    return a, aB

                acc_only0 = [False]
                ret_w_pre = {}

                def inproj_feat(c, wb, wbB, w0, ntok=512):
                    a, aB = acc[0] if acc_only0[0] else acc[evac_ctr[0] % 2]
                    evac_ctr[0] += 1
                    tiles = hT_all[c * 4:c * 4 + ntok // 128]
                    for k in range(8):
                        tk.op(T_, lambda e, k=k, a=a: e.matmul(a[:, 0:ntok], lhsT=wb[:, k, w0:w0 + 128],
                                                               rhs=hT[:, k, c * 512:c * 512 + ntok],
                                                               start=(k == 0), stop=(k == 7)),
                              reads=tiles + wbB, writes=[aB], inc=(k == 7))
                    return a, aB

                with ExitStack() as sN:
                    vaug = sbt(sN, "vaug", [128, NT, 4, 194], BF16)
                    qTs = [sbt(sN, "qT%d" % i, [128, L], BF16) for i in range(2)]
                    kTs = [sbt(sN, "kT%d" % i, [128, LT], BF16) for i in range(2)]
                    sgTs = [sbt(sN, "sgT%d" % i, [128, L], BF16) for i in range(2)]
                    ths = [sbt(sN, "th%d" % i, [128, 512], F32) for i in range(2)]
                    eS = [sbt(sN, "eS%d" % i, [128, 2, 256], BF16) for i in range(NSC)]
                    PT = [sbt(sN, "PT%d" % i, [128, 2, 256], BF16) for i in range(NSC + 1)]
                    rdr = [sbt(sN, "rdr%d" % i, [128, 256], F32) for i in range(2)]
                    rd = [sbt(sN, "rd%d" % i, [128, 256], F32) for i in range(2)]
                    tts = [sbt(sN, "tt%d" % i, [128, 256], F32) for i in range(2)]
                    sc3 = [pbank[2], pbank[3], pbank[7], pbank[1]][:NSC]
                    tk.op(P_, lambda e: e.memset(vaug[:, :, :, :], 0.0), writes=[buf("vaug_all")])
                    tk.op(P_, lambda e: e.memset(vaug[:, :, :, 64:65], 1.0), writes=[buf("vaug_all")])
                    tk.op(P_, lambda e: e.memset(vaug[:, :, :, 66:67], 1.0), writes=[buf("vaug_all")])
                    for i in range(2):
                        tk.op(P_, lambda e, i=i: e.memset(rd[i][:, :], 1.0), writes=[buf("rd%d" % i)])
                        tk.op(P_, lambda e, i=i: e.memset(rdr[i][:, :], 1.0), writes=[buf("rdr%d" % i)])

                    wb, wbB = wgv
                    for t in range(NT):
                        a, aB = inproj_tok(t, wb, wbB, 512)
                        av = a[:, :].rearrange("p (g h d) -> p g h d", g=4, h=2)
                        tk.op(S_, lambda e, t=t, av=av: e.copy(out=vaug[:, t, :, 0:64], in_=av[:, :, 0, :]),
                              reads=[aB, buf("vaug_all")], writes=[buf("vaug%d" % t)])
                        tk.op(V, lambda e, t=t, av=av: e.tensor_copy(out=vaug[:, t, :, 130:194], in_=av[:, :, 1, :]),
                              reads=[aB, buf("vaug_all")], writes=[buf("vaug%d" % t)])

                    def pair_chunks(p):
                        pb_ = p % 2
                        qT, kT, sgT = qTs[pb_], kTs[pb_], sgTs[pb_]
                        holder = {}
                        units = []

                        def u_load():
                            wb, wbB = next_w()
                            holder["w"] = (wb, wbB)
                            for j, c0 in enumerate([p * 128, 512 + p * 128, 1536 + p * 128]):
                                tk.dma(P_, wb[:, :, j * 128:(j + 1) * 128],
                                       win_d[:, c0:c0 + 128].rearrange("(k p) n -> p k n", p=128), writes=[wbB[j]])
                        units.append(u_load)

                        def mk_q(c):
                            def u():
                                wb, wbB = holder["w"]
                                a, aB = inproj_feat(c, wb, wbB, 0)
                                tk.op(S_, lambda e: e.activation(out=qT[:, c * 512:(c + 1) * 512], in_=a[:, :],
                                                                 func=AF.Copy, scale=0.125),
                                      reads=[aB], writes=[buf("qT%d_%d" % (pb_, c))])
                            return u

                        def mk_k(c):
                            def u():
                                wb, wbB = holder["w"]
                                ntok = 512 if c < 4 else 256
                                a, aB = inproj_feat(c, wb, wbB, 128, ntok)
                                if c % 2 == 0:
                                    tk.op(S_, lambda e: e.copy(out=kT[:, c * 512:c * 512 + ntok], in_=a[:, 0:ntok]),
                                          reads=[aB], writes=[buf("kT%d_%d" % (pb_, c))])
                                else:
                                    tk.op(V, lambda e: e.tensor_copy(out=kT[:, c * 512:c * 512 + ntok], in_=a[:, 0:ntok]),
                                          reads=[aB], writes=[buf("kT%d_%d" % (pb_, c))])
                            return u

                        def mk_g(c):
                            def u():
                                wb, wbB = holder["w"]
                                a, aB = inproj_feat(c, wb, wbB, 256)
                                th, thB = ths[c % 2], buf("th%d" % (c % 2))
                                tk.op(S_, lambda e: e.activation(out=th[:, :], in_=a[:, :], func=AF.Tanh, scale=0.5),
                                      reads=[aB], writes=[thB])
                                tk.op(V, lambda e: e.scalar_tensor_tensor(out=sgT[:, c * 512:(c + 1) * 512], in0=th[:, :],
                                                                          scalar=1.0, in1=a[:, :], op0=ALU.add, op1=ALU.mult),
                                      reads=[thB, aB], writes=[buf("sgT%d_%d" % (pb_, c))])
                            return u
                        for c in range(5):
                            units.append(mk_k(c))
                        for c in range(4):
                            units.append(mk_q(c))
                            units.append(mk_g(c))
                        return units

                    gctr = [0]

                    def attention(p, inject):
                        pb_ = p % 2
                        qT, kT, sgT = qTs[pb_], kTs[pb_], sgTs[pb_]
                        G = []
                        for i in range(8):
                            if i == 0:
                                Rs = [0, 2, 4, 6]
                            elif i == 7:
                                Rs = [24, 26, 28, 30]
                            else:
                                Rs = [4 * i - 4 + 2 * t for t in range(6)]
                            if i in (0, 7):
                                groups = [[(Rs[1], 0, 4, 0), (Rs[0], 0, 4, 256)], [(Rs[3], 0, 4, 0), (Rs[2], 0, 4, 256)]]
                            else:
                                groups = [[(Rs[1], 0, 4, 0), (Rs[0], 0, 2, 256)],
                                          [(Rs[3], 0, 4, 0), (Rs[2], 0, 4, 256)],
                                          [(Rs[5], 3, 4, 0), (Rs[4], 1, 4, 64)]]
                            groups = groups + [None]
                            nmm_head = sum(2 for _ in groups)
                            for hh in range(2):
                                mm0 = 0
                                for gi, grp in enumerate(groups):
                                    G.append(dict(i=i, hh=hh, grp=grp, gi=gi, ng=len(groups), mm0=mm0, nmm=nmm_head,
                                                  last=(hh == 1 and gi == len(groups) - 1)))
                                    mm0 += 2
                        for g in G:
                            c_ = gctr[0]
                            gctr[0] += 1
                            g["sc"] = sc3[c_ % NSC]
                            g["eS"] = (eS[c_ % NSC], buf("eS%d" % (c_ % NSC)))
                            g["PT"] = (PT[c_ % (NSC + 1)], buf("PT%d" % (c_ % (NSC + 1))))
                            g["PTb"] = buf("PTb%d" % (c_ % (NSC + 1)))

                        def front(g):
                            i, hh, grp = g["i"], g["hh"], g["grp"]
                            hs = slice(hh * 64, hh * 64 + 64)
                            hidx = 2 * p + hh
                            Mt = Mtab[0] if i in (0, 7) else Mtab[1]
                            s_t, sB_ = g["sc"]
                            e_t, eB = g["eS"]
                            p_t, pB = g["PT"]
                            qB = buf("qT%d_%d" % (pb_, i // 2))
                            if grp is None:
                                tiles = [(2048, 0, 4, 0), (2176, 0, 4, 256)]
                            else:
                                tiles = [(R * 64, qlo, qhi, pos) for (R, qlo, qhi, pos) in grp]
                            g["tiles"] = tiles
                            kbufs = [buf("kT%d_%d" % (pb_, kt // 512)) for (kt, _, _, _) in tiles]
                            ncol = max(pos + (qhi - qlo) * 64 for (_, qlo, qhi, pos) in tiles)
                            e_f = e_t[:, :, :].rearrange("p a q -> p (a q)")
                            p_f = p_t[:, :, :].rearrange("p a q -> p (a q)")
                            for j, (kt, qlo, qhi, pos) in enumerate(tiles):
                                nq = (qhi - qlo) * 64
                                tk.op(T_, lambda e, kt=kt, qlo=qlo, nq=nq, pos=pos: e.matmul(
                                    s_t[:, pos:pos + nq], lhsT=kT[hs, kt:kt + 128],
                                    rhs=qT[hs, i * 256 + qlo * 64:i * 256 + qlo * 64 + nq], start=True, stop=True),
                                    reads=kbufs + [qB], writes=[sB_], inc=(j == 1))
                            if grp is None:
                                tk.op(S_, lambda e: e.activation(out=p_f[:, 0:ncol], in_=s_t[:, 0:ncol], func=AF.Exp),
                                      reads=[sB_], writes=[pB, g["PTb"]])
                            else:
                                tk.op(S_, lambda e: e.activation(out=e_f[:, 0:ncol], in_=s_t[:, 0:ncol], func=AF.Exp),
                                      reads=[sB_], writes=[eB])
                                for j, (kt, qlo, qhi, pos) in enumerate(tiles):
                                    nq = (qhi - qlo) * 64
                                    s0 = 7 - (kt // 64 - 4 * i)
                                    tk.op(V, lambda e, qlo=qlo, qhi=qhi, nq=nq, pos=pos, s0=s0: e.tensor_tensor(
                                        out=p_f[:, pos:pos + nq], in0=e_f[:, pos:pos + nq],
                                        in1=Mt[:, hidx, s0 + qlo:s0 + qhi, :].rearrange("p s q -> p (s q)"), op=ALU.mult),
                                        reads=[eB, buf("Mtab")], writes=[pB if j == 0 else g["PTb"]])

                        def back(g):
                            i, hh = g["i"], g["hh"]
                            o_t, oB = ob[i % 2]
                            p_t, pB = g["PT"]
                            p_f = p_t[:, :, :].rearrange("p a q -> p (a q)")
                            nmm = g["nmm"]
                            for j, (kt, qlo, qhi, pos) in enumerate(g["tiles"]):
                                nq = (qhi - qlo) * 64
                                tile = kt // 128
                                imm = g["mm0"] + j
                                if hh == 0:
                                    o_dst = o_t[0:65, qlo * 64:qlo * 64 + nq]
                                    vcols = slice(0, 65)
                                else:
                                    o_dst = o_t[:, 256 + qlo * 64:256 + qlo * 64 + nq]
                                    vcols = slice(66, 194)
                                tk.op(T_, lambda e, tile=tile, imm=imm, o_dst=o_dst, vcols=vcols, pos=pos, nq=nq: e.matmul(
                                    o_dst, lhsT=vaug[:, tile, p, vcols], rhs=p_f[:, pos:pos + nq],
                                    start=(imm == 0), stop=(imm == nmm - 1)),
                                    reads=[pB, g["PTb"], buf("vaug%d" % tile)], writes=[oB], inc=(imm == nmm - 1))

                        def fin1(i):
                            o_t, oB = ob[i % 2]
                            r_r, r_rB = rdr[i % 2], buf("rdr%d" % (i % 2))
                            r_d, r_dB = rd[i % 2], buf("rd%d" % (i % 2))
                            tk.op(S_, lambda e: e.copy(out=r_r[64:65, :], in_=o_t[64:65, 0:256]), reads=[oB], writes=[r_rB])
                            tk.op(S_, lambda e: e.copy(out=r_r[0:1, :], in_=o_t[0:1, 256:512]), reads=[oB], writes=[r_rB])
                            tk.op(V, lambda e: e.reciprocal(out=r_d[0:65, :], in_=r_r[0:65, :]),
                                  reads=[r_rB], writes=[r_dB])

                        def fin2(i):
                            o_t, oB = ob[i % 2]
                            r_d, r_dB = rd[i % 2], buf("rd%d" % (i % 2))
                            tt, ttB = tts[i % 2], buf("tt%d" % (i % 2))
                            tk.op(T_, lambda e: e.matmul(bcb[:, 0:256], lhsT=oeo[:, :], rhs=r_d[:, :], start=True, stop=True),
                                  reads=[r_dB, buf("oeo")], writes=[bcB])
                            tk.op(V, lambda e: e.tensor_tensor(out=tt[:, :], in0=sgT[:, i * 256:(i + 1) * 256],
                                                               in1=bcb[:, 0:256], op=ALU.mult),
                                  reads=[buf("sgT%d_%d" % (pb_, i // 2)), bcB], writes=[ttB])
                            tk.op(V, lambda e: e.tensor_tensor(out=yT[0:64, p, i * 256:(i + 1) * 256],
                                                               in0=o_t[0:64, 0:256], in1=tt[0:64, :], op=ALU.mult),
                                  reads=[oB, ttB], writes=[buf("yT%d_%d" % (p, i))])
                            tk.op(V, lambda e: e.tensor_tensor(out=yT[64:128, p, i * 256:(i + 1) * 256],
                                                               in0=o_t[64:128, 256:512], in1=tt[64:128, :], op=ALU.mult),
                                  reads=[oB, ttB], writes=[buf("yT%d_%d" % (p, i))])

                        n = len(G)
                        pend = []
                        inj = list(inject)
                        stride = max(1, n // (len(inj) + 1)) if inj else n + 1
                        SK = NSC - 1
                        for k_ in range(SK):
                            front(G[k_])
                        for idx in range(n):
                            if idx + SK < n:
                                front(G[idx + SK])
                            back(G[idx])
                            if G[idx]["last"]:
                                fin1(G[idx]["i"])
                                pend.append((idx + FIN_DELAY, G[idx]["i"]))
                            while pend and pend[0][0] <= idx:
                                fin2(pend.pop(0)[1])
                            if inj and idx % stride == stride - 1:
                                inj.pop(0)()
                        for _, i in pend:
                            fin2(i)
                        for u in inj:
                            u()

                    if stop == "NA2":
                        pass
                    units = pair_chunks(0)
                    for u in units:
                        u()
                    acc_only0[0] = (NSC == 4)
                    for p in range(4 if stop != "NA1" else 0):
                        nxt = pair_chunks(p + 1) if p < 3 else []
                        if p == 3:
                            wbr, wbrB = next_w()
                            for j, c0 in enumerate([2048, 2560, 3072, 3584]):
                                tk.dma(P_, wbr[:, :, j * 128:(j + 1) * 128],
                                       win_d[:, c0:c0 + 128].rearrange("(k p) n -> p k n", p=128), writes=[wbrB[j]])
                            ret_w_pre[0] = (wbr, wbrB)
                        if INTERLEAVE_NA:
                            attention(p, nxt)
                        else:
                            attention(p, [])
                            for u in nxt:
                                u()
                    barrier()
                with ExitStack() as sR:
                    nheads_ret = 0 if stop in ("NA", "NA1", "NA2") else 4
                    qks = [sbt(sR, "qk%d" % i, [128, 16, 2, 128], BF16) for i in range(2)]
                    vts = [sbt(sR, "vt%d" % i, [128, 16, 128], BF16) for i in range(2)]
                    sgs = [sbt(sR, "sg%d" % i, [128, 16, 128], BF16) for i in range(2)]
                    kcxs = [sbt(sR, "kcx%d" % i, [128, 2, 128], BF16) for i in range(2)]
                    vcfs = [sbt(sR, "vcf%d" % i, [128, 2, 128], BF16) for i in range(2)]
                    vcbs = [sbt(sR, "vcb%d" % i, [128, 2, 128], BF16) for i in range(2)]
                    Sb16s = [sbt(sR, "Sb16_%d" % i, [128, 16, 128], BF16) for i in range(2)]
                    Sf32s = [sbt(sR, "Sf32_%d" % i, [128, 128], F32) for i in range(2)]
                    Sb32s = [sbt(sR, "Sb32_%d" % i, [128, 128], F32) for i in range(2)]
                    Sf16 = [sbt(sR, "Sf16_%d" % i, [128, 128], BF16) for i in range(5)]
                    T1s = [sbt(sR, "T1_%d" % i, [128, 2, 2, 64], F32) for i in range(2)]
                    Us = [sbt(sR, "U_%d" % i, [128, 2, 2, 64], F32) for i in range(2)]
                    th2s = [sbt(sR, "th2_%d" % i, [128, 128], F32) for i in range(2)]
                    junkR = sbt(sR, "junkR", [128, 128], F32)
                    ssR = sbt(sR, "ssR", [128, 4, 4], F32)
                    NTB = 4
                    tmpb = {nm: [sbt(sR, "%s%d" % (nm, i), [128, 128], BF16) for i in range(NTB)]
                            for nm in ("qT_", "kT_", "qfT", "qbT", "AT", "vwf", "vwb", "yt")}

                    def tb(nm, n):
                        return tmpb[nm][n % NTB], buf("%s%d" % (nm, n % NTB))

                    def pass1_units(h):
                        hb_ = h % 2
                        qk, vt, sg, kcx, vcf, vcb = qks[hb_], vts[hb_], sgs[hb_], kcxs[hb_], vcfs[hb_], vcbs[hb_]
                        holder = {}
                        units = []

                        def u_load():
                            if h in ret_w_pre:
                                holder["w"] = ret_w_pre.pop(h)
                                return
                            wb, wbB = next_w()
                            holder["w"] = (wb, wbB)
                            for j, c0 in enumerate([2048, 2560, 3072, 3584]):
                                tk.dma(P_, wb[:, :, j * 128:(j + 1) * 128],
                                       win_d[:, c0 + h * 128:c0 + (h + 1) * 128].rearrange("(k p) n -> p k n", p=128),
                                       writes=[wbB[j]])
                        units.append(u_load)

                        def mk_tile(t):
                            def u():
                                wb, wbB = holder["w"]
                                a, aB = inproj_tok(t, wb, wbB, 512)
                                T1, T1B = T1s[t % 2], buf("T1_%d" % (t % 2))
                                U, U0B, U1B = Us[t % 2], buf("U0_%d" % (t % 2)), buf("U1_%d" % (t % 2))
                                th2, th2B = th2s[t % 2], buf("th2_%d" % (t % 2))
                                av = a[:, 0:256].rearrange("p (a h d) -> p a h d", a=2, h=2)
                                cosb = cossin[:, 0, t:t + 1, :].unsqueeze(1).to_broadcast([128, 2, 2, 64])
                                sinb = cossin[:, 1, t:t + 1, :].to_broadcast([128, 2, 64])
                                tk.op(V, lambda e: e.tensor_tensor(out=T1[:, :, :, :], in0=av, in1=cosb, op=ALU.mult),
                                      reads=[aB, buf("cossin")], writes=[T1B])
                                tk.op(V, lambda e: e.tensor_tensor(out=U[:, :, 0, :], in0=av[:, :, 1, :], in1=sinb, op=ALU.mult),
                                      reads=[aB, buf("cossin")], writes=[U0B])
                                tk.op(V, lambda e: e.tensor_tensor(out=U[:, :, 1, :], in0=av[:, :, 0, :], in1=sinb, op=ALU.mult),
                                      reads=[aB, buf("cossin")], writes=[U1B])
                                tk.op(P_, lambda e: e.tensor_tensor(out=qk[:, t, :, 0:64], in0=T1[:, :, 0, :], in1=U[:, :, 0, :],
                                                                    op=ALU.subtract),
                                      reads=[T1B, U0B], writes=[buf("qk%d_%d" % (hb_, t))])
                                tk.op(P_, lambda e: e.tensor_tensor(out=qk[:, t, :, 64:128], in0=T1[:, :, 1, :], in1=U[:, :, 1, :],
                                                                    op=ALU.add),
                                      reads=[T1B, U1B], writes=[buf("qk%d_%d" % (hb_, t))])
                                tk.op(S_, lambda e: e.copy(out=vt[:, t, :], in_=a[:, 256:384]),
                                      reads=[aB], writes=[buf("vt%d_%d" % (hb_, t))])
                                if RET_SILU:
                                    tk.op(S_, lambda e: e.activation(out=sg[:, t, :], in_=a[:, 384:512], func=AF.Silu),
                                          reads=[aB], writes=[buf("sg%d_%d" % (hb_, t))])
                                else:
                                    tk.op(S_, lambda e: e.activation(out=th2[:, :], in_=a[:, 384:512], func=AF.Tanh, scale=0.5),
                                          reads=[aB], writes=[th2B])
                                    tk.op(V, lambda e: e.scalar_tensor_tensor(out=sg[:, t, :], in0=th2[:, :], scalar=1.0,
                                                                              in1=a[:, 384:512], op0=ALU.add, op1=ALU.mult),
                                          reads=[th2B, aB], writes=[buf("sg%d_%d" % (hb_, t))])
                            return u

                        def mk_ctx(t):
                            def u():
                                wb, wbB = holder["w"]
                                a, aB = inproj_tok(t, wb, wbB, 256, c0=128)
                                ci = t - 16
                                cf_col = rcols[:, h, 2:3] if ci == 0 else rcols[:, h, 0:1]
                                cb_col = rcols[:, h, 1:2] if ci == 0 else rcols[:, h, 3:4]
                                tk.op(S_, lambda e: e.copy(out=kcx[:, ci, :], in_=a[:, 0:128]),
                                      reads=[aB], writes=[buf("kcx%d" % hb_)])
                                tk.op(V, lambda e: e.tensor_scalar(out=vcf[:, ci, :], in0=a[:, 128:256], scalar1=cf_col,
                                                                   scalar2=None, op0=ALU.mult),
                                      reads=[aB, buf("rcols")], writes=[buf("vcf%d" % hb_)])
                                tk.op(V, lambda e: e.tensor_scalar(out=vcb[:, ci, :], in0=a[:, 128:256], scalar1=cb_col,
                                                                   scalar2=None, op0=ALU.mult),
                                      reads=[aB, buf("rcols")], writes=[buf("vcb%d" % hb_)])
                            return u

                        def u_state():
                            s_t, sB_ = pbank[3]
                            for ci in range(2):
                                tk.op(T_, lambda e, ci=ci: e.matmul(s_t[:, 0:128], lhsT=kcx[:, ci, :], rhs=vcf[:, ci, :],
                                                                    start=(ci == 0), stop=(ci == 1)),
                                      reads=[buf("kcx%d" % hb_), buf("vcf%d" % hb_)], writes=[sB_], inc=(ci == 1))
                            for ci in range(2):
                                tk.op(T_, lambda e, ci=ci: e.matmul(s_t[:, 128:256], lhsT=kcx[:, ci, :], rhs=vcb[:, ci, :],
                                                                    start=(ci == 0), stop=(ci == 1)),
                                      reads=[buf("kcx%d" % hb_), buf("vcb%d" % hb_)], writes=[sB_], inc=(ci == 1))
                            tk.op(V, lambda e: e.tensor_copy(out=Sf32s[hb_][:, :], in_=s_t[:, 0:128]),
                                  reads=[sB_], writes=[buf("Sf32_%d" % hb_)])
                            tk.op(V, lambda e: e.tensor_copy(out=Sb32s[hb_][:, :], in_=s_t[:, 128:256]),
                                  reads=[sB_], writes=[buf("Sb32_%d" % hb_)])
                        units.append(mk_ctx(16))
                        units.append(mk_ctx(17))
                        units.append(u_state)
                        for t in range(16):
                            units.append(mk_tile(t))
                        return units

                    def sweep_units(h, bwd_banks=(2, 3)):
                        hb_ = h % 2
                        qk, vt, sg, Sb16 = qks[hb_], vts[hb_], sgs[hb_], Sb16s[hb_]
                        Sf32, Sb32 = Sf32s[hb_], Sb32s[hb_]
                        Sf32B, Sb32B = buf("Sf32_%d" % hb_), buf("Sb32_%d" % hb_)
                        qkB = lambda n: buf("qk%d_%d" % (hb_, n))
                        vtB = lambda n: buf("vt%d_%d" % (hb_, n))
                        units = []

                        def B1(n):
                            if n <= 0:
                                return
                            vwb, vwbB = tb("vwb", n)
                            tk.op(S_, lambda e: e.activation(out=vwb[:, :], in_=vt[:, n, :], func=AF.Copy,
                                                             scale=rcols[:, h, 1:2]),
                                  reads=[vtB(n), buf("rcols")], writes=[vwbB])

                        def B2(n):
                            tk.op(S_, lambda e: e.copy(out=Sb16[:, n, :], in_=Sb32[:, :]),
                                  reads=[Sb32B], writes=[buf("Sb16_%d_%d" % (hb_, n))])
                            if n == 0:
                                return
                            vwb, vwbB = tb("vwb", n)
                            s_t, sB_ = pbank[bwd_banks[n % 2]]
                            tk.op(T_, lambda e: e.matmul(s_t[:, 256:384], lhsT=qk[:, n, 1, :], rhs=vwb[:, :],
                                                         start=True, stop=True),
                                  reads=[qkB(n), vwbB], writes=[sB_])
                            tk.op(V, lambda e: e.scalar_tensor_tensor(out=Sb32[:, :], in0=Sb32[:, :], scalar=rcols[:, h, 5:6],
                                                                      in1=s_t[:, 256:384], op0=ALU.mult, op1=ALU.add),
                                  reads=[Sb32B, sB_, buf("rcols")], writes=[Sb32B])

                        def mk_bwd(n):
                            def u():
                                if n == 15:
                                    B1(15)
                                    B1(14)
                                B1(n - 2)
                                B2(n)
                            return u
                        for n in range(15, -1, -1):
                            units.append(mk_bwd(n))
                        bwd_list = units
                        units = []

                        NSF = 5

                        def F0():
                            tk.op(S_, lambda e: e.copy(out=Sf16[0][:, :], in_=Sf32[:, :]),
                                  reads=[Sf32B], writes=[buf("Sf16_0")])

                        def U1(n):
                            if n >= 15:
                                return
                            vwf, vwfB = tb("vwf", n)
                            tk.op(S_, lambda e: e.activation(out=vwf[:, :], in_=vt[:, n, :], func=AF.Copy, scale=rcols[:, h, 0:1]),
                                  reads=[vtB(n), buf("rcols")], writes=[vwfB])

                        def F1a(n):
                            tp_, tpB_ = tp2[0]
                            for j in range(2):
                                tk.op(T_, lambda e, j=j: e.transpose(out=tp_[:, j, :], in_=qk[:, n, j, :], identity=ident[:, :]),
                                      reads=[qkB(n), buf("ident")], writes=[tpB_], inc=(j == 1))
                            qT_, qTB = tb("qT_", n)
                            kT_, kTB = tb("kT_", n)
                            qfT, qfB = tb("qfT", n)
                            qbT, qbB = tb("qbT", n)
                            tk.op(S_, lambda e: e.copy(out=qT_[:, :], in_=tp_[:, 0, :]), reads=[tpB_], writes=[qTB])
                            tk.op(V, lambda e: e.tensor_copy(out=kT_[:, :], in_=tp_[:, 1, :]), reads=[tpB_], writes=[kTB])
                            tk.op(P_, lambda e: e.tensor_tensor(out=qfT[:, :], in0=qT_[:, :], in1=tabq[:, h, 0, :], op=ALU.mult),
                                  reads=[qTB, buf("tabq")], writes=[qfB])
                            tk.op(P_, lambda e: e.tensor_tensor(out=qbT[:, :], in0=qT_[:, :], in1=tabq[:, h, 1, :], op=ALU.mult),
                                  reads=[qTB, buf("tabq")], writes=[qbB])

                        def F1b(n):
                            qT_, qTB = tb("qT_", n)
                            kT_, kTB = tb("kT_", n)
                            AT, ATB = tb("AT", n)
                            s_t, sB_ = pbank[2]
                            tk.op(T_, lambda e: e.matmul(s_t[:, 0:128], lhsT=kT_[:, :], rhs=qT_[:, :], start=True, stop=True),
                                  reads=[kTB, qTB], writes=[sB_])
                            tk.op(V, lambda e: e.tensor_tensor(out=AT[:, :], in0=s_t[:, 0:128], in1=DT[:, h, :], op=ALU.mult),
                                  reads=[sB_, buf("DT")], writes=[ATB])

                        def U_(n):
                            if n >= 15:
                                return
                            vwf, vwfB = tb("vwf", n)
                            s_t, sB_ = pbank[3]
                            tk.op(T_, lambda e: e.matmul(s_t[:, 0:128], lhsT=qk[:, n, 1, :], rhs=vwf[:, :], start=True, stop=True),
                                  reads=[qkB(n), vwfB], writes=[sB_])
                            tk.op(V, lambda e: e.scalar_tensor_tensor(out=Sf32[:, :], in0=Sf32[:, :], scalar=rcols[:, h, 4:5],
                                                                      in1=s_t[:, 0:128], op0=ALU.mult, op1=ALU.add),
                                  reads=[Sf32B, sB_, buf("rcols")], writes=[Sf32B])
                            tk.op(S_, lambda e: e.copy(out=Sf16[(n + 1) % NSF][:, :], in_=Sf32[:, :]),
                                  reads=[Sf32B], writes=[buf("Sf16_%d" % ((n + 1) % NSF))])

                        def F2a(n):
                            qfT, qfB = tb("qfT", n)
                            qbT, qbB = tb("qbT", n)
                            AT, ATB = tb("AT", n)
                            o_t, oB = ob[n % 2]
                            sf16, sf16B = Sf16[n % NSF], buf("Sf16_%d" % (n % NSF))
                            c = n % 4
                            sB_ = buf("ssR%d" % c)
                            tk.op(T_, lambda e: e.matmul(o_t[:, 0:128], lhsT=AT[:, :], rhs=vt[:, n, :], start=True, stop=False),
                                  reads=[ATB, vtB(n)], writes=[oB], inc=False)
                            tk.op(T_, lambda e: e.matmul(o_t[:, 0:128], lhsT=qfT[:, :], rhs=sf16[:, :], start=False, stop=False),
                                  reads=[qfB, sf16B], writes=[oB], inc=False)
                            tk.op(T_, lambda e: e.matmul(o_t[:, 0:128], lhsT=qbT[:, :], rhs=Sb16[:, n, :], start=False, stop=True),
                                  reads=[qbB, buf("Sb16_%d_%d" % (hb_, n))], writes=[oB])
                            tk.op(S_, lambda e: e.activation(out=junkR[:, :], in_=o_t[:, 0:128], func=AF.Square,
                                                             accum_out=ssR[:, c, 0:1]),
                                  reads=[oB], writes=[buf("junkR"), sB_])
                            tk.op(V, lambda e: e.tensor_scalar(out=ssR[:, c, 1:2], in0=ssR[:, c, 0:1], scalar1=1.0 / 128,
                                                               scalar2=EPS, op0=ALU.mult, op1=ALU.add),
                                  reads=[sB_], writes=[sB_])
                            tk.op(P_, lambda e: e.tensor_tensor(out=ssR[:, c, 2:3], in0=ssR[:, c, 1:2], in1=cneg[:, 0:1], op=ALU.pow),
                                  reads=[sB_, buf("cneg")], writes=[sB_])

                        def F2b(n):
                            yt, ytB = tb("yt", n)
                            o_t, oB = ob[n % 2]
                            c = n % 4
                            sB_ = buf("ssR%d" % c)
                            tk.op(V, lambda e: e.scalar_tensor_tensor(out=yt[:, :], in0=o_t[:, 0:128], scalar=ssR[:, c, 2:3],
                                                                      in1=sg[:, n, :], op0=ALU.mult, op1=ALU.mult),
                                  reads=[oB, sB_, buf("sg%d_%d" % (hb_, n))], writes=[ytB])

                        def F3(n):
                            yt, ytB = tb("yt", n)
                            tp_, tpB_ = tp2[1]
                            tk.op(T_, lambda e: e.transpose(out=tp_[:, 0, :], in_=yt[:, :], identity=ident[:, :]),
                                  reads=[ytB, buf("ident")], writes=[tpB_])
                            tk.op(S_, lambda e: e.copy(out=yT[:, 4 + h, n * 128:(n + 1) * 128], in_=tp_[:, 0, :]),
                                  reads=[tpB_], writes=[buf("yTr%d_%d" % (h, n))])

                        def mk_fwd(s_):
                            def u():
                                if s_ == 0:
                                    F0()
                                    U1(0)
                                if s_ < 16:
                                    U1(s_ + 1)
                                    F1a(s_)
                                    U_(s_)
                                if 0 <= s_ - 1 < 16:
                                    F1b(s_ - 1)
                                if 0 <= s_ - 2 < 16:
                                    F2a(s_ - 2)
                                if 0 <= s_ - 3 < 16:
                                    F2b(s_ - 3)
                                if 0 <= s_ - 4 < 16:
                                    F3(s_ - 4)
                            return u
                        for s_ in range(20):
                            units.append(mk_fwd(s_))
                        return bwd_list, units

                    wsts = [sbt(sR, "wst%d" % i, [128, D], F32) for i in range(2)]
                    rngh = sbt(sR, "rngh", [128, 4], F32)

                    def wo_view(k, c0, c1):
                        return hT[:, k // 2, (k % 2) * 1024 + c0:(k % 2) * 1024 + c1]

                    def wout_units():
                        units = []

                        def u0():
                            tk.op(V, lambda e: e.tensor_scalar(out=rngh[:, :], in0=rngc[:, :], scalar1=0.5, scalar2=None,
                                                               op0=ALU.mult),
                                  reads=[buf("rngc")], writes=[buf("rngh")])
                        units.append(u0)

                        def mk(k):
                            def u():
                                w_s, w_sB = wsts[k % 2], buf("wst%d" % (k % 2))
                                tk.dma(Y_, w_s[:, :], wout_d[k * 128:(k + 1) * 128, :], writes=[w_sB])
                                sc1 = 0.5 if k < 4 else (rngc[:, k - 4:k - 3] if RET_SILU else rngh[:, k - 4:k - 3])
                                tk.op(V, lambda e: e.tensor_scalar(out=wo_view(k, 0, 1024), in0=w_s[:, :], scalar1=sc1,
                                                                   scalar2=None, op0=ALU.mult),
                                      reads=[w_sB, buf("rngh")], writes=hT_all[0:16] + [buf("wo")])
                            return u
                        for k in range(8):
                            units.append(mk(k))
                        return units

                    def interleave(*lists):
                        lists = [list(l) for l in lists if l]
                        if not lists:
                            return
                        nmax = max(len(l) for l in lists)
                        pos = [0] * len(lists)
                        for step in range(nmax):
                            for li, l in enumerate(lists):
                                tgt = ((step + 1) * len(l) + nmax - 1) // nmax
                                while pos[li] < min(tgt, len(l)):
                                    l[pos[li]]()
                                    pos[li] += 1

                    def prefetch_ret_w(h):
                        wbr, wbrB = next_w()
                        for j, c0 in enumerate([2048, 2560, 3072, 3584]):
                            tk.dma(P_, wbr[:, :, j * 128:(j + 1) * 128],
                                   win_d[:, c0 + h * 128:c0 + (h + 1) * 128].rearrange("(k p) n -> p k n", p=128),
                                   writes=[wbrB[j]])
                        ret_w_pre[h] = (wbr, wbrB)

                    if nheads_ret == 4:
                        if RET_SCHED == 0:
                            interleave(pass1_units(0))
                            for h in range(4):
                                bw, fw = sweep_units(h)
                                pu = pass1_units(h + 1) if h + 1 < 4 else wout_units()
                                interleave(bw + fw, pu)
                        else:
                            prefetch_ret_w(1)
                            interleave(pass1_units(0))
                            bw0, fw0 = sweep_units(0)
                            interleave(bw0, pass1_units(1))
                            bw1, fw1 = sweep_units(1, bwd_banks=(0, 1))
                            prefetch_ret_w(2)
                            interleave(fw0, bw1)
                            p1_2 = pass1_units(2)
                            prefetch_ret_w(3)
                            interleave(fw1, p1_2)
                            bw2, fw2 = sweep_units(2)
                            interleave(bw2, pass1_units(3))
                            bw3, fw3 = sweep_units(3, bwd_banks=(0, 1))
                            interleave(fw2, bw3)
                            interleave(fw3, wout_units())
                    barrier()
            if stop in ("NA", "RET", "NA1", "NA2"):
                if "yT" in dbg and b == 0:
                    with ExitStack() as sd:
                        dbg32 = sbt(sd, "dbgy", [128, 2048], F32)
                        for k in range(4 if stop == "NA" else 8):
                            tk.op(V, lambda e, k=k: e.tensor_copy(out=dbg32[:, :], in_=yT[:, k, :]), writes=[buf("dbgy")])
                            tk.dma(Y_, dbg_out["yT"][:, k * L:(k + 1) * L], dbg32[:, :], reads=[buf("dbgy")],
                                   writes=[buf("dbg_yT")])
                        barrier()
                return
            with ExitStack() as sD:
                xs2 = [sbt(sD, "xsD%d" % i, [128, D], F32) for i in range(3)]
                ost = [sbt(sD, "ost%d" % i, [128, D], F32) for i in range(2)]
                r32s = [sbt(sD, "r32_%d" % i, [128, D], F32) for i in range(4)]
                junk2 = sbt(sD, "junkD", [128, D], BF16)
                gate_t = sbt(sD, "gate_t", [128, D], F32)
                fng_t = sbt(sD, "fng_t", [128, D], F32)
                tk.dma(Y_, gate_t[:, :], mod_scr[b:b + 1, 2 * D:3 * D].to_broadcast([128, D]),
                       reads=[buf("mod_scr_g")], writes=[buf("gate_t")])
                tk.dma(Y_, fng_t[:, :], fng_d[0:1, :].to_broadcast([128, D]), writes=[buf("fng_t")])
                yT_all = [bb for nm, bb in B.items() if nm.startswith("yT")]
                xs3 = xs2
                if "wo" in dbg and b == 0:
                    for k in range(8):
                        tk.op(V, lambda e, k=k: e.tensor_copy(out=r32s[0][:, :], in_=wo_view(k, 0, 1024)),
                              reads=[buf("wo")] + hT_all[0:16], writes=[buf("r32_0")])
                        tk.dma(Y_, dbg_out["wo"][:, k * D:(k + 1) * D], r32s[0][:, :], reads=[buf("r32_0")],
                               writes=[buf("dbg_wo")])
                ssD = sbt(sD, "ssD", [128, 4, 4], F32)

                def D_s0(t):
                    xsb, xB = xs3[t % 3], buf("xsD%d" % (t % 3))
                    tk.dma(Y_, xsb[:, :], x_d[b, t * 128:(t + 1) * 128, :], writes=[xB])
                    for half in range(2):
                        a, aB = pbank[(t % 2) * 2 + half]
                        for k in range(8):
                            tk.op(T_, lambda e, k=k, a=a, half=half: e.matmul(
                                a[:, :], lhsT=yT[:, k, t * 128:(t + 1) * 128], rhs=wo_view(k, half * 512, (half + 1) * 512),
                                start=(k == 0), stop=(k == 7)),
                                reads=yT_all + [buf("wo")] + hT_all[0:16], writes=[aB], inc=(k == 7))

                NR = 4

                def rbuf(t):
                    return r32s[t % NR], buf("r32_%d" % (t % NR)), buf("r32h_%d" % (t % NR))

                def D_s1a(t):
                    r, rB, rhB = rbuf(t)
                    for half in range(2):
                        a, aB = pbank[(t % 2) * 2 + half]
                        tk.op(V, lambda e, a=a, half=half: e.tensor_tensor(out=r[:, half * 512:(half + 1) * 512], in0=a[:, :],
                                                                           in1=gate_t[:, half * 512:(half + 1) * 512],
                                                                           op=ALU.mult),
                              reads=[aB, buf("gate_t")], writes=[rB if half == 0 else rhB])

                def D_s1b(t):
                    xsb, xB = xs3[t % 3], buf("xsD%d" % (t % 3))
                    r, rB, rhB = rbuf(t)
                    c = t % 4
                    tk.op(P_, lambda e: e.tensor_tensor(out=r[:, 0:512], in0=r[:, 0:512], in1=xsb[:, 0:512], op=ALU.add),
                          reads=[rB, xB], writes=[rB])
                    tk.op(V, lambda e: e.tensor_tensor(out=r[:, 512:1024], in0=r[:, 512:1024], in1=xsb[:, 512:1024], op=ALU.add),
                          reads=[rhB, xB], writes=[rhB])
                    tk.op(S_, lambda e: e.activation(out=junk2[:, :], in_=r[:, :], func=AF.Square, accum_out=ssD[:, c, 0:1]),
                          reads=[rB, rhB], writes=[buf("junkD"), buf("ssD%d" % c)])

                def D_s2a(t):
                    c = t % 4
                    sB_ = buf("ssD%d" % c)
                    tk.op(V, lambda e: e.tensor_scalar(out=ssD[:, c, 1:2], in0=ssD[:, c, 0:1], scalar1=1.0 / D, scalar2=EPS,
                                                       op0=ALU.mult, op1=ALU.add),
                          reads=[sB_], writes=[sB_])
                    tk.op(P_, lambda e: e.tensor_tensor(out=ssD[:, c, 2:3], in0=ssD[:, c, 1:2], in1=cneg[:, 0:1], op=ALU.pow),
                          reads=[sB_, buf("cneg")], writes=[sB_])
                    r, rB, rhB = rbuf(t)
                    tk.op(P_, lambda e: e.tensor_tensor(out=r[:, 0:512], in0=r[:, 0:512], in1=fng_t[:, 0:512], op=ALU.mult),
                          reads=[rB, buf("fng_t")], writes=[rB])
                    tk.op(V, lambda e: e.tensor_tensor(out=r[:, 512:1024], in0=r[:, 512:1024], in1=fng_t[:, 512:1024],
                                                       op=ALU.mult),
                          reads=[rhB, buf("fng_t")], writes=[rhB])

                def D_s2b(t):
                    r, rB, rhB = rbuf(t)
                    c = t % 4
                    sB_ = buf("ssD%d" % c)
                    o_s, o_sB = ost[t % 2], buf("ost%d" % (t % 2))
                    tk.op(S_, lambda e: e.activation(out=o_s[:, :], in_=r[:, :], func=AF.Copy, scale=ssD[:, c, 2:3]),
                          reads=[rB, rhB, sB_], writes=[o_sB])
                    tk.dma(Y_, out_d[b, t * 128:(t + 1) * 128, :], o_s[:, :], reads=[o_sB])

                for step in range(16 + 4):
                    if step < 16:
                        D_s0(step)
                    if 0 <= step - 4 < 16:
                        D_s2b(step - 4)
                    if 0 <= step - 2 < 16:
                        D_s1b(step - 2)
                    if 0 <= step - 1 < 16:
                        D_s1a(step - 1)
                    if 0 <= step - 3 < 16:
                        D_s2a(step - 3)
            barrier()

        for b_ in range(nb):
            do_batch(b_)

        for bb in list(B.values()):
            if bb.dsem is not None and bb.dcnt > 0:
                tk.wait_ev(Y_, (bb.dsem, bb.dcnt, None))
        tk.emit()
    return nc


def _prep_inputs(inputs, nb=NB, ncores=8):
    f = np.float32
    x = np.ascontiguousarray(inputs["x"], dtype=f)
    ctx = np.ascontiguousarray(inputs["ctx"], dtype=f)
    c = np.asarray(inputs["c"], dtype=f)
    c_ctx = np.asarray(inputs["c_ctx"], dtype=f)
    shared = {
        "norm_g": np.ascontiguousarray(inputs["norm_g"][0:1], dtype=f),
        "w_ada": np.ascontiguousarray(inputs["w_ada"][0], dtype=f),
        "b_ada": np.ascontiguousarray(inputs["b_ada"][0:1], dtype=f),
        "w_in": np.ascontiguousarray(inputs["w_in"][0], dtype=f),
        "w_out": np.ascontiguousarray(inputs["w_out"][0], dtype=f),
        "final_norm_g": np.ascontiguousarray(np.asarray(inputs["final_norm_g"], dtype=f)[None, :]),
        "rng_col": np.ascontiguousarray(np.asarray(inputs["ret_norm_g"][0], dtype=f).reshape(4, 128).T),
        "decay8": np.ascontiguousarray(np.concatenate([np.asarray(inputs["ret_decay_fwd"][0], dtype=f),
                                                       np.asarray(inputs["ret_decay_bwd"][0], dtype=f)])[None, :]),
        "bias_tab": _bias_table(np.asarray(inputs["na_rpb"][0], dtype=f)),
    }
    shared.update(_consts())
    maps = []
    for i in range(ncores):
        m = dict(shared)
        m["x"] = x[i * nb:(i + 1) * nb]
        m["ctx"] = ctx[i * nb:(i + 1) * nb]
        rows = [c[i * nb + j] for j in range(nb)]
        while len(rows) < 2:
            rows.append(rows[-1])
        m["c3"] = np.ascontiguousarray(np.stack(rows + [c_ctx], axis=0))
        maps.append(m)
    return maps


def kernel(**inputs):
    nc = build_nc()
    maps = _prep_inputs(inputs)
    res = run_bass_kernel_spmd(nc, maps, core_ids=list(range(8)))
    return np.concatenate([r["out"] for r in res.results], axis=0)
```

```python
from contextlib import ExitStack
import numpy as np
import concourse.bass as bass
import concourse.mybir as mybir
from concourse.bass_utils import run_bass_kernel_spmd

F32 = mybir.dt.float32
BF16 = mybir.dt.bfloat16
AF = mybir.ActivationFunctionType
ALU = mybir.AluOpType
AX = mybir.AxisListType

D = 1024
L = 2048
LC = 256
LT = L + LC
NT = LT // 128
GW = 64
EPS = 1e-6
NB = 2
CH = 128
INTERLEAVE_NA = True
FIN_DELAY = 5
RET_SCHED = 1
NSC = 4
RET_SILU = True
ATTACH_WAITS = True
TRANSITIVE_WAITS = True


class Buf:
    __slots__ = ("name", "w", "r", "dsem", "dcnt", "excl")

    def __init__(self, name, excl=False):
        self.name = name
        self.excl = excl
        self.w = None
        self.r = {}
        self.dsem = None
        self.dcnt = 0


class TK:
    ENGS = ("tensor", "vector", "scalar", "gpsimd", "sync")

    def __init__(self, nc, stack):
        self.nc = nc
        self.stack = stack
        self.esem = {e: stack.enter_context(nc.semaphore("es_" + e)) for e in self.ENGS}
        self.ecnt = {e: 0 for e in self.ENGS}
        self.seen = {e: {} for e in self.ENGS}
        self.prog = {e: [] for e in self.ENGS}
        self.nsem = 0
        self.clock = {}
        self.ev_seq = {}
        self.seq = 0
        self.same_engine_sync = True
        self.same_engine_war = True

    def _collect(self, eng, reads, writes):
        need = {}

        def add(ev, raw=True):
            if ev is None:
                return
            sem, val, src = ev
            if src == eng and (eng == "tensor" or not self.same_engine_sync or (not raw and not self.same_engine_war)):
                return
            if need.get(sem, 0) < val:
                need[sem] = val

        for b in reads:
            add(b.w)
            if b.excl:
                for ev in b.r.values():
                    if ev[2] != eng:
                        add(ev)
        for b in writes:
            add(b.w, raw=False)
            for ev in b.r.values():
                add(ev, raw=False)
        seen = self.seen[eng]
        order = sorted(need.items(), key=lambda kv: -self.ev_seq.get((kv[0], kv[1]), 0))
        for sem, val in order:
            if seen.get(sem, 0) >= val:
                continue
            seen[sem] = val
            self.prog[eng].append(("wait", sem, val))
            clk = self.clock.get((sem, val))
            if clk is not None and TRANSITIVE_WAITS:
                for s2, v2 in clk.items():
                    if seen.get(s2, 0) < v2:
                        seen[s2] = v2

    def _record(self, ev, reads, writes):
        sem = ev[0]
        for b in reads:
            old = b.r.get(sem)
            if old is None or old[1] < ev[1]:
                b.r[sem] = ev
        for b in writes:
            b.w = ev
            b.r = {}

    def op(self, eng, fn, reads=(), writes=(), inc=True):
        self._collect(eng, reads, writes)
        self.seq += 1
        if inc:
            self.ecnt[eng] += 1
            ev = (self.esem[eng], self.ecnt[eng], eng)
            clk = dict(self.seen[eng])
            if eng == "tensor" or self.ecnt[eng] > 1:
                pass
            self.clock[(ev[0], ev[1])] = clk
        else:
            ev = (self.esem[eng], self.ecnt[eng] + 1, eng)
        self.ev_seq[(ev[0], ev[1])] = self.seq
        self.prog[eng].append(("op", fn, inc))
        self._record(ev, reads, writes)

    def dma(self, eng, out, in_, reads=(), writes=(), **kw):
        b = writes[0] if writes else reads[0]
        if b.dsem is None:
            b.dsem = self.stack.enter_context(self.nc.semaphore("ds%d" % self.nsem))
            self.nsem += 1
        self._collect(eng, reads, writes)
        if b.dcnt > 0 and self.seen[eng].get(b.dsem, 0) < b.dcnt:
            self.seen[eng][b.dsem] = b.dcnt
            self.prog[eng].append(("wait", b.dsem, b.dcnt))
        b.dcnt += 16
        ev = (b.dsem, b.dcnt, None)
        self.seq += 1
        self.clock[(ev[0], ev[1])] = dict(self.seen[eng])
        self.ev_seq[(ev[0], ev[1])] = self.seq
        self.prog[eng].append(("dma", out, in_, b.dsem, kw))
        self._record(ev, reads, writes)
        return ev

    def barrier(self, bufs):
        for e in self.ENGS:
            for e2 in self.ENGS:
                if (e2 != e or e != "tensor") and self.ecnt[e2] > self.seen[e].get(self.esem[e2], 0):
                    self.seen[e][self.esem[e2]] = self.ecnt[e2]
                    self.prog[e].append(("wait", self.esem[e2], self.ecnt[e2]))
            for b in bufs:
                if b.dsem is not None and b.dcnt > self.seen[e].get(b.dsem, 0):
                    self.seen[e][b.dsem] = b.dcnt
                    self.prog[e].append(("wait", b.dsem, b.dcnt))

    def wait_ev(self, eng, ev):
        sem, val, _ = ev
        if self.seen[eng].get(sem, 0) < val:
            self.seen[eng][sem] = val
            self.prog[eng].append(("wait", sem, val))

    def replay(self, eng, e):
        sem = self.esem[eng]
        pend = None
        for it in self.prog[eng]:
            if it[0] == "wait":
                if pend is not None:
                    e.wait_ge(pend[1], pend[2])
                pend = it if ATTACH_WAITS else None
                if not ATTACH_WAITS:
                    e.wait_ge(it[1], it[2])
            elif it[0] == "op":
                ins = it[1](e)
                if pend is not None:
                    ins._wait_ge(pend[1], pend[2])
                    pend = None
                if it[2]:
                    ins.then_inc(sem, 1)
            else:
                if pend is not None:
                    e.wait_ge(pend[1], pend[2])
                    pend = None
                e.dma_start(out=it[1], in_=it[2], **it[4]).then_inc(it[3], 16)
        if pend is not None:
            e.wait_ge(pend[1], pend[2])

    def emit(self):
        nc = self.nc
        with nc.Block() as block:
            @block.tensor
            def _(e):
                self.replay("tensor", e)

            @block.vector
            def _(e):
                self.replay("vector", e)

            @block.scalar
            def _(e):
                self.replay("scalar", e)

            @block.gpsimd
            def _(e):
                self.replay("gpsimd", e)

            @block.sync
            def _(e):
                self.replay("sync", e)


KSCALE = float(128 ** -0.5)
NSLOT = 16


def _consts():
    f = np.float32
    c = {}
    c["ident"] = np.eye(128, dtype=f)
    tt = np.arange(L)
    row = (tt // GW).astype(np.float64)
    col = (tt % GW).astype(np.float64)
    nf = 32
    inv = 10000.0 ** (-np.arange(nf, dtype=np.float64) / nf)
    ang = np.concatenate([row[:, None] * inv, col[:, None] * inv], axis=-1)
    cs = np.stack([np.cos(ang).astype(f), np.sin(ang).astype(f)], axis=0)
    c["cossin"] = np.ascontiguousarray(cs.reshape(2, 16, 128, 64).transpose(2, 0, 1, 3).reshape(128, 2 * 16 * 64))
    p = np.arange(128)
    kp, kc = p // 64, p % 64
    qc = np.arange(64)
    c0 = np.clip(qc - 8, 0, 48)
    colv = (kc[:, None] >= c0[None, :]) & (kc[:, None] < c0[None, :] + 16)
    sl = np.arange(NSLOT)
    delta = 7 + kp[:, None] - sl[None, :]
    okf = (delta >= -7) & (delta <= 7) & (sl[None, :] >= 1) & (sl[None, :] <= 14)
    oki = okf & (delta >= -4) & (delta <= 3)
    c["cm_full"] = np.ascontiguousarray((okf[:, :, None] & colv[:, None, :]).astype(f).reshape(128, NSLOT * 64))
    c["cm_int"] = np.ascontiguousarray((oki[:, :, None] & colv[:, None, :]).astype(f).reshape(128, NSLOT * 64))
    s_ = np.arange(128)[:, None].astype(f)
    t_ = np.arange(128)[None, :].astype(f)
    dposF = np.maximum(t_ - s_, 0)
    dposB = np.maximum(s_ - t_, 0)
    maskF = (t_ > s_).astype(f) * f(KSCALE)
    maskB = (s_ > t_).astype(f) * f(KSCALE)
    eye2 = np.eye(128, dtype=f) * f(2.0 * KSCALE)
    rt1 = np.broadcast_to(t_ + 1, (128, 128)).astype(f)
    rt2 = np.broadcast_to(128 - t_, (128, 128)).astype(f)
    cols = np.stack([127 - s_[:, 0], s_[:, 0], 255 - s_[:, 0], 128 + s_[:, 0]], axis=1).astype(f)
    c["rconst"] = np.ascontiguousarray(np.concatenate([dposF, dposB, maskF, maskB, eye2, rt1, rt2, cols], axis=1))
    oeo = np.zeros((128, 128), f)
    oeo[64, 0:64] = 1.0
    oeo[0, 64:128] = 1.0
    c["ones_eo"] = oeo
    return c


def _bias_table(rpb):
    p = np.arange(128)
    kp, kc = p // 64, p % 64
    sl = np.arange(NSLOT)
    qc = np.arange(64)
    ri = np.clip(14 + kp[:, None] - sl[None, :], 0, 14)
    ci = np.clip(kc[:, None] - qc[None, :] + 15, 0, 30)
    tab = rpb[:, ri[:, :, None], ci[:, None, :]]
    return np.ascontiguousarray(tab.transpose(1, 0, 2, 3).reshape(128, 8 * NSLOT * 64).astype(np.float32))


def build_nc(nb=NB, debug=None, stop=None):
    nc = bass.Bass("TRN2", target_bir_lowering=False)
    dbg = debug or ()

    def din(name, shape):
        return nc.dram_tensor(name, list(shape), F32, kind="ExternalInput").ap()

    x_d = din("x", [nb, L, D])
    ctx_d = din("ctx", [nb, LC, D])
    c_d = din("c3", [3, D])
    normg_d = din("norm_g", [1, D])
    wada_d = din("w_ada", [D, 3 * D])
    bada_d = din("b_ada", [1, 3 * D])
    win_d = din("w_in", [D, 4096])
    wout_d = din("w_out", [D, D])
    fng_d = din("final_norm_g", [1, D])
    rngc_d = din("rng_col", [128, 4])
    decay_d = din("decay8", [1, 8])
    btab_d = din("bias_tab", [128, 8 * NSLOT * 64])
    ident_d = din("ident", [128, 128])
    cossin_d = din("cossin", [128, 2 * 16 * 64])
    cmf_d = din("cm_full", [128, NSLOT * 64])
    cmi_d = din("cm_int", [128, NSLOT * 64])
    rconst_d = din("rconst", [128, 7 * 128 + 4])
    oeo_d = din("ones_eo", [128, 128])
    out_d = nc.dram_tensor("out", [nb, L, D], F32, kind="ExternalOutput").ap()

    dbg_out = {}
    if "hT" in dbg:
        dbg_out["hT"] = nc.dram_tensor("dbg_hT", [128, 8 * LT], F32, kind="ExternalOutput").ap()
    if "yT" in dbg:
        dbg_out["yT"] = nc.dram_tensor("dbg_yT", [128, 8 * L], F32, kind="ExternalOutput").ap()
    if "wo" in dbg:
        dbg_out["wo"] = nc.dram_tensor("dbg_wo", [128, 8 * D], F32, kind="ExternalOutput").ap()
    if "mod" in dbg:
        dbg_out["mod"] = nc.dram_tensor("dbg_mod", [3, 3 * D], F32, kind="ExternalOutput").ap()

    mod_scr = nc.dram_tensor("mod_scr", [3, 3 * D], F32).ap()

    V, S_, P_, T_, Y_ = "vector", "scalar", "gpsimd", "tensor", "sync"

    with ExitStack() as st:
        tk = TK(nc, st)
        B = {}

        def buf(name, excl=False):
            if name not in B:
                B[name] = Buf(name, excl)
            return B[name]

        uniq = [0]

        def sbt(stack, name, shape, dt):
            uniq[0] += 1
            return stack.enter_context(nc.sbuf_tensor("sb%d_%s" % (uniq[0], name), list(shape), dt))

        def sb(name, shape, dt):
            return sbt(st, name, shape, dt)

        def barrier():
            tk.barrier(list(B.values()))

        def psb(name, shape, dt):
            t = st.enter_context(nc.psum_tensor("ps_" + name, list(shape), dt))
            return t, buf("ps_" + name, excl=True)

        pbank = [psb("b%d" % i, [128, 512], F32) for i in range(8)]
        acc = pbank[0:2]
        sc = pbank[2:4]
        ob = pbank[4:6]
        bcb, bcB = pbank[6]
        tpB = pbank[7][1]
        tpb = pbank[7][0][:, :].bitcast(BF16).rearrange("p (a b) -> p a b", a=8)
        tpb2 = pbank[6][0][:, :].bitcast(BF16).rearrange("p (a b) -> p a b", a=8)
        tp2 = [(tpb, tpB), (tpb2, bcB)]

        ident = sb("ident", [128, 128], BF16)
        hT = sb("hT", [128, 8, LT], BF16)
        yT = sb("yT", [128, 8, L], BF16)
        ss = sb("ss", [128, 16], F32)
        cneg = sb("cneg", [128, 4], F32)
        Mtab = [sb("Mfull", [128, 8, NSLOT, 64], BF16), sb("Mint", [128, 8, NSLOT, 64], BF16)]
        cossin = sb("cossin", [128, 2, 16, 64], F32)
        oeo = sb("oeo", [128, 128], F32)
        lg = sb("lg", [128, 8], F32)
        DT = sb("DT", [128, 4, 128], F32)
        tabq = sb("tabq", [128, 4, 2, 128], F32)
        rcols = sb("rcols", [128, 4, 8], F32)
        rngc = sb("rngc", [128, 4], F32)

        wbuf = [sb("wbuf%d" % i, [128, 8, 512], BF16) for i in range(2)]
        wbufB = [[buf("wbuf%d_s%d" % (i, j)) for j in range(4)] for i in range(2)]
        wctr = [0]
        tk.op(P_, lambda e: e.memset(cneg[:, :], -0.5), writes=[buf("cneg")])

        s0 = ExitStack()
        if True:
            stg = sbt(s0, "stg", [128, 256], F32)
            stg2 = sbt(s0, "stg2", [128, 1024], F32)
            cmf = sbt(s0, "cmf", [128, NSLOT * 64], F32)
            cmi = sbt(s0, "cmi", [128, NSLOT * 64], F32)
            rconst = sbt(s0, "rconst", [128, 7 * 128 + 4], F32)
            c3T = sbt(s0, "c3T", [128, 3, 8], F32)
            c3s = sbt(s0, "c3s", [128, 8, 4], BF16)
            mod3 = [sbt(s0, "mod3_%d" % i, [3, 256], F32) for i in range(2)]
            bada3 = [sbt(s0, "bada3_%d" % i, [3, 256], F32) for i in range(2)]
            dec8 = sbt(s0, "dec8", [128, 8], F32)

            for r in range(3):
                tk.dma(Y_, c3T[:, r, :], c_d[r, :].rearrange("(k p) -> p k", p=128), writes=[buf("c3T%d" % r)],
                       allow_slow_non_contiguous=True)
            tk.dma(Y_, stg[:, 0:128], ident_d[:, :], writes=[buf("stg")])
            tk.op(V, lambda e: e.tensor_copy(out=ident[:, :], in_=stg[:, 0:128]),
                  reads=[buf("stg")], writes=[buf("ident")])
            tk.dma(Y_, cossin[:, :, :, :].rearrange("p a t d -> p (a t d)"), cossin_d[:, :], writes=[buf("cossin")])
            tk.dma(Y_, oeo[:, :], oeo_d[:, :], writes=[buf("oeo")])
            tk.dma(Y_, rngc[:, :], rngc_d[:, :], writes=[buf("rngc")])
            tk.dma(Y_, cmf[:, :], cmf_d[:, :], writes=[buf("cmf")])
            tk.dma(Y_, cmi[:, :], cmi_d[:, :], writes=[buf("cmi")])
            tk.dma(Y_, rconst[:, :], rconst_d[:, :], writes=[buf("rconst")])
            tk.dma(Y_, dec8[:, :], decay_d[0:1, :].to_broadcast([128, 8]), writes=[buf("dec8")])

            c3v = c3T[:, :, :].rearrange("p r k -> p (r k)")
            c3sv = c3s[:, :, 0:3].rearrange("p k r -> p r k")
            c3B = [buf("c3T%d" % r) for r in range(3)]
            tk.op(S_, lambda e: e.activation(out=stg2[:, 0:24], in_=c3v, func=AF.Tanh, scale=0.5),
                  reads=c3B, writes=[buf("stg2")])
            tk.op(V, lambda e: e.scalar_tensor_tensor(out=stg2[:, 32:56], in0=stg2[:, 0:24], scalar=1.0, in1=c3v,
                                                      op0=ALU.add, op1=ALU.mult),
                  reads=[buf("stg2")] + c3B, writes=[buf("stg2b")])
            tk.op(V, lambda e: e.tensor_scalar(out=c3sv, in0=stg2[:, 32:56].rearrange("p (r k) -> p r k", r=3),
                                               scalar1=0.5, scalar2=None, op0=ALU.mult),
                  reads=[buf("stg2b")], writes=[buf("c3s")])
            for J in range(6):
                wb, wbBl = wbuf[J % 2], wbufB[J % 2]
                pm, pmB = pbank[6 + (J % 2)]
                tk.dma(P_, wb[:, :, :], wada_d[:, J * 512:(J + 1) * 512].rearrange("(k p) n -> p k n", p=128),
                       writes=wbBl)
                for k in range(8):
                    tk.op(T_, lambda e, k=k, wb=wb, pm=pm: e.matmul(pm[0:3, 0:512], lhsT=c3s[:, k, 0:3], rhs=wb[:, k, :],
                                                                     start=(k == 0), stop=(k == 7)),
                          reads=[buf("c3s")] + wbBl, writes=[pmB], inc=(k == 7))
                for half in range(2):
                    j = 2 * J + half
                    m3, m3B = mod3[j % 2], buf("mod3_%d" % (j % 2))
                    b3, b3B = bada3[j % 2], buf("bada3_%d" % (j % 2))
                    tk.dma(Y_, b3[:, :], bada_d[0:1, j * 256:(j + 1) * 256].to_broadcast([3, 256]), writes=[b3B])
                    tk.op(V, lambda e, pm=pm, m3=m3, b3=b3, half=half: e.tensor_tensor(
                        out=m3[:, :], in0=pm[0:3, half * 256:(half + 1) * 256], in1=b3[:, :], op=ALU.add),
                        reads=[pmB, b3B], writes=[m3B])
                    tk.dma(Y_, mod_scr[:, j * 256:(j + 1) * 256], m3[:, :], reads=[m3B],
                           writes=[buf("mod_scr" if j < 8 else "mod_scr_g")])

        def setup_part2(stA, stAB, stE, stEB):
            tk.op(S_, lambda e: e.activation(out=dec8[:, :], in_=dec8[:, :], func=AF.Exp),
                  reads=[buf("dec8")], writes=[buf("dec8")])
            tk.op(V, lambda e: e.tensor_scalar(out=lg[:, :], in0=dec8[:, :], scalar1=-1.0, scalar2=None, op0=ALU.mult),
                  reads=[buf("dec8")], writes=[buf("lg")])
            RC = lambda i: rconst[:, i * 128:(i + 1) * 128]
            lnk = float(np.log(KSCALE))

            def ret_consts(h):
                lf, lb = lg[:, h:h + 1], lg[:, 4 + h:5 + h]
                sA = stA[:, (h % 2) * 256:(h % 2) * 256 + 128]
                sB2 = stA[:, (h % 2) * 256 + 128:(h % 2) * 256 + 256]
                hB = buf("stA_h%d" % (h % 2))
                deps = [stAB, hB]
                tk.op(S_, lambda e: e.activation(out=sA, in_=RC(0), func=AF.Exp, scale=lf),
                      reads=[buf("rconst"), buf("lg")], writes=deps)
                tk.op(S_, lambda e: e.activation(out=sB2, in_=RC(1), func=AF.Exp, scale=lb),
                      reads=[buf("rconst"), buf("lg")], writes=[hB])
                tk.op(V, lambda e: e.tensor_tensor(out=sA, in0=sA, in1=RC(2), op=ALU.mult),
                      reads=[hB, buf("rconst")], writes=[hB])
                tk.op(V, lambda e: e.tensor_tensor(out=sB2, in0=sB2, in1=RC(3), op=ALU.mult),
                      reads=[hB, buf("rconst")], writes=[hB])
                tk.op(V, lambda e: e.tensor_tensor(out=sA, in0=sA, in1=sB2, op=ALU.add),
                      reads=[hB], writes=[hB])
                tk.op(V, lambda e: e.tensor_tensor(out=DT[:, h, :], in0=sA, in1=RC(4), op=ALU.add),
                      reads=[hB, buf("rconst")], writes=[buf("DT"), stAB])
                tk.op(S_, lambda e: e.activation(out=tabq[:, h, 0, :], in_=RC(5), func=AF.Exp, scale=lf, bias=lnk),
                      reads=[buf("rconst"), buf("lg")], writes=[buf("tabq")])
                tk.op(S_, lambda e: e.activation(out=tabq[:, h, 1, :], in_=RC(6), func=AF.Exp, scale=lb, bias=lnk),
                      reads=[buf("rconst"), buf("lg")], writes=[buf("tabq")])
                cc = lambda i: rconst[:, 7 * 128 + i:7 * 128 + i + 1]
                for j, (src, sc_) in enumerate([(cc(0), lf), (cc(1), lb), (cc(2), lf), (cc(3), lb),
                                                (rconst[:, 6 * 128:6 * 128 + 1], lf),
                                                (rconst[:, 6 * 128:6 * 128 + 1], lb)]):
                    tk.op(S_, lambda e, j=j, src=src, sc_=sc_: e.activation(out=rcols[:, h, j:j + 1], in_=src,
                                                                            func=AF.Exp, scale=sc_),
                          reads=[buf("rconst"), buf("lg")], writes=[buf("rcols")])

            def na_table(h):
                sE, sEB = stE[h % 2], stEB[h % 2]
                tk.dma(Y_, sE[:, :], btab_d[:, h * 1024:(h + 1) * 1024], writes=[sEB])
                tk.op(S_, lambda e: e.activation(out=sE[:, :], in_=sE[:, :], func=AF.Exp),
                      reads=[sEB], writes=[sEB])
                tk.op(V, lambda e: e.tensor_tensor(out=Mtab[0][:, h, :, :].rearrange("p s q -> p (s q)"),
                                                   in0=sE[:, :], in1=cmf[:, :], op=ALU.mult),
                      reads=[sEB, buf("cmf")], writes=[buf("Mtab")])
                tk.op(P_, lambda e: e.tensor_tensor(out=Mtab[1][:, h, :, :].rearrange("p s q -> p (s q)"),
                                                    in0=sE[:, :], in1=cmi[:, :], op=ALU.mult),
                      reads=[sEB, buf("cmi")], writes=[buf("Mtab")])

            units = []
            for h in range(8):
                units.append(lambda h=h: na_table(h))
                if h < 4:
                    units.append(lambda h=h: ret_consts(h))
            return units

        if stop == "setup":
            nb = 0


        def next_w():
            i = wctr[0] % 2
            wctr[0] += 1
            return wbuf[i], wbufB[i]

        def do_batch(b):
            wgv = next_w()
            with ExitStack() as sA:
                xs = [sbt(sA, "xsA%d" % i, [128, D], F32) for i in range(3)]
                junk = sbt(sA, "junkA", [128, D], BF16)
                t32s = [sbt(sA, "t32A%d" % i, [128, D], F32) for i in range(2)]
                hbs = [sbt(sA, "hbA%d" % i, [128, D], BF16) for i in range(2)]
                A_t = sbt(sA, "A_t", [128, D], F32)
                sh_t = sbt(sA, "sh_t", [128, D], F32)
                A_c = sbt(sA, "A_c", [128, D], F32)
                sh_c = sbt(sA, "sh_c", [128, D], F32)
                g_t = sbt(sA, "g_t", [128, D], F32)
                ssA = sbt(sA, "ssA", [128, 4, 4], F32)
                tk.dma(Y_, g_t[:, :], normg_d[0:1, :].to_broadcast([128, D]), writes=[buf("g_t")])

                def prep_mod(row, At, Sh, nm):
                    tk.dma(Y_, At[:, :], mod_scr[row:row + 1, D:2 * D].to_broadcast([128, D]),
                           reads=[buf("mod_scr")], writes=[buf("A_" + nm)])
                    tk.dma(Y_, Sh[:, :], mod_scr[row:row + 1, 0:D].to_broadcast([128, D]),
                           reads=[buf("mod_scr")], writes=[buf("sh_" + nm)])
                    tk.op(V, lambda e: e.scalar_tensor_tensor(out=At[:, :], in0=At[:, :], scalar=1.0, in1=g_t[:, :],
                                                              op0=ALU.add, op1=ALU.mult),
                          reads=[buf("A_" + nm), buf("g_t")], writes=[buf("A_" + nm)])


                def A_s0(t):
                    xsb, xB = xs[t % 3], buf("xsA%d" % (t % 3))
                    src = x_d[b, t * 128:(t + 1) * 128, :] if t < 16 else ctx_d[b, (t - 16) * 128:(t - 15) * 128, :]
                    tk.dma(Y_, xsb[:, :], src, writes=[xB])

                A_s0(0)
                A_s0(1)
                prep_mod(b, A_t, sh_t, "t")
                p2_units = []
                if b == 0:
                    scr0 = sbt(sA, "scrA0", [128, 512], F32)
                    p2_units = setup_part2(scr0, buf("scrA0"), [A_c, sh_c], [buf("A_c"), buf("sh_c")])

                def A_s1(t):
                    xsb, xB = xs[t % 3], buf("xsA%d" % (t % 3))
                    c = t % 4
                    sB_ = buf("ssA%d" % c)
                    tk.op(S_, lambda e: e.activation(out=junk[:, :], in_=xsb[:, :], func=AF.Square,
                                                     accum_out=ssA[:, c, 0:1]),
                          reads=[xB], writes=[buf("junk"), sB_])
                    tk.op(V, lambda e: e.tensor_scalar(out=ssA[:, c, 1:2], in0=ssA[:, c, 0:1], scalar1=1.0 / D, scalar2=EPS,
                                                       op0=ALU.mult, op1=ALU.add),
                          reads=[sB_], writes=[sB_])
                    tk.op(P_, lambda e: e.tensor_tensor(out=ssA[:, c, 2:3], in0=ssA[:, c, 1:2], in1=cneg[:, 0:1], op=ALU.pow),
                          reads=[sB_, buf("cneg")], writes=[sB_])

                def A_s2(t):
                    xsb, xB = xs[t % 3], buf("xsA%d" % (t % 3))
                    c = t % 4
                    t32, t32B = t32s[t % 2], buf("t32A%d" % (t % 2))
                    hb, hbB = hbs[t % 2], buf("hbA%d" % (t % 2))
                    At, Sh, nm = (A_t, sh_t, "t") if t < 16 else (A_c, sh_c, "c")
                    tk.op(V, lambda e: e.scalar_tensor_tensor(out=t32[:, :], in0=xsb[:, :], scalar=ssA[:, c, 2:3],
                                                              in1=At[:, :], op0=ALU.mult, op1=ALU.mult),
                          reads=[xB, buf("ssA%d" % c), buf("A_" + nm)], writes=[t32B])

                def A_s2b(t):
                    t32, t32B = t32s[t % 2], buf("t32A%d" % (t % 2))
                    hb, hbB = hbs[t % 2], buf("hbA%d" % (t % 2))
                    At, Sh, nm = (A_t, sh_t, "t") if t < 16 else (A_c, sh_c, "c")
                    tk.op(P_, lambda e: e.tensor_tensor(out=hb[:, 0:512], in0=t32[:, 0:512], in1=Sh[:, 0:512], op=ALU.add),
                          reads=[t32B, buf("sh_" + nm)], writes=[hbB])
                    tk.op(V, lambda e: e.tensor_tensor(out=hb[:, 512:1024], in0=t32[:, 512:1024], in1=Sh[:, 512:1024],
                                                       op=ALU.add),
                          reads=[t32B, buf("sh_" + nm)], writes=[buf("hbA%dh" % (t % 2))])

                def A_s3(t):
                    hb, hbB = hbs[t % 2], buf("hbA%d" % (t % 2))
                    tp_, tpB_ = tp2[t % 2]
                    for k in range(8):
                        tk.op(T_, lambda e, k=k: e.transpose(out=tp_[:, k, :], in_=hb[:, k * 128:(k + 1) * 128],
                                                             identity=ident[:, :]),
                              reads=[hbB, buf("hbA%dh" % (t % 2)), buf("ident")], writes=[tpB_], inc=(k == 7))
                    tk.op(S_, lambda e: e.copy(out=hT[:, :, t * 128:(t + 1) * 128], in_=tp_[:, :, :]),
                          reads=[tpB_], writes=[buf("hT%d" % t)])

                for step in range(NT + 4):
                    if step == 6:
                        tk.dma(P_, wgv[0][:, :, :], win_d[:, 1024:1536].rearrange("(k p) n -> p k n", p=128),
                               writes=wgv[1])
                    if step == 15:
                        assert not p2_units
                        prep_mod(2, A_c, sh_c, "c")
                    if step >= 2 and p2_units:
                        p2_units.pop(0)()
                    if 2 <= step < NT:
                        A_s0(step)
                    if 0 <= step - 4 < NT:
                        A_s3(step - 4)
                    if 0 <= step - 3 < NT:
                        A_s2b(step - 3)
                    if 0 <= step - 2 < NT:
                        A_s2(step - 2)
                    if 0 <= step - 1 < NT:
                        A_s1(step - 1)
                for u in p2_units:
                    u()
                barrier()
            if b == 0:
                s0.close()
            hT_all = [buf("hT%d" % t) for t in range(NT)]

            if "hT" in dbg and b == 0:
                with ExitStack() as sd:
                    dbg32 = sbt(sd, "dbg32", [128, 1152], F32)
                    for k in range(8):
                        for hh in range(0, LT, 1152):
                            tk.op(V, lambda e, k=k, hh=hh: e.tensor_copy(out=dbg32[:, :], in_=hT[:, k, hh:hh + 1152]),
                                  reads=hT_all, writes=[buf("dbg32")])
                            tk.dma(Y_, dbg_out["hT"][:, k * LT + hh:k * LT + hh + 1152], dbg32[:, :],
                                   reads=[buf("dbg32")], writes=[buf("dbg_hT")])
                    barrier()
            if stop == "A":
                return

            with ExitStack() as sB:

                evac_ctr = [0]

                def inproj_tok(t, wb, wbB, ncol, c0=0):
                    a, aB = acc[evac_ctr[0] % 2]
                    evac_ctr[0] += 1
                    for k in range(8):
                        tk.op(T_, lambda e, k=k, a=a: e.matmul(a[:, 0:ncol], lhsT=hT[:, k, t * 128:(t + 1) * 128],
                                                               rhs=wb[:, k, c0:c0 + ncol], start=(k == 0), stop=(k == 7)),
                              reads=[hT_all[t]] + wbB, writes=[aB], inc=(k == 7))
                    return a, aB

                acc_only0 = [False]
                ret_w_pre = {}

                def inproj_feat(c, wb, wbB, w0, ntok=512):
                    a, aB = acc[0] if acc_only0[0] else acc[evac_ctr[0] % 2]
                    evac_ctr[0] += 1
                    tiles = hT_all[c * 4:c * 4 + ntok // 128]
                    for k in range(8):
                        tk.op(T_, lambda e, k=k, a=a: e.matmul(a[:, 0:ntok], lhsT=wb[:, k, w0:w0 + 128],
                                                               rhs=hT[:, k, c * 512:c * 512 + ntok],
                                                               start=(k == 0), stop=(k == 7)),
                              reads=tiles + wbB, writes=[aB], inc=(k == 7))
                    return a, aB

                with ExitStack() as sN:
                    vaug = sbt(sN, "vaug", [128, NT, 4, 194], BF16)
                    qTs = [sbt(sN, "qT%d" % i, [128, L], BF16) for i in range(2)]
                    kTs = [sbt(sN, "kT%d" % i, [128, LT], BF16) for i in range(2)]
                    sgTs = [sbt(sN, "sgT%d" % i, [128, L], BF16) for i in range(2)]
                    ths = [sbt(sN, "th%d" % i, [128, 512], F32) for i in range(2)]
                    eS = [sbt(sN, "eS%d" % i, [128, 2, 256], BF16) for i in range(NSC)]
                    PT = [sbt(sN, "PT%d" % i, [128, 2, 256], BF16) for i in range(NSC + 1)]
                    rdr = [sbt(sN, "rdr%d" % i, [128, 256], F32) for i in range(2)]
                    rd = [sbt(sN, "rd%d" % i, [128, 256], F32) for i in range(2)]
                    tts = [sbt(sN, "tt%d" % i, [128, 256], F32) for i in range(2)]
                    sc3 = [pbank[2], pbank[3], pbank[7], pbank[1]][:NSC]
                    tk.op(P_, lambda e: e.memset(vaug[:, :, :, :], 0.0), writes=[buf("vaug_all")])
                    tk.op(P_, lambda e: e.memset(vaug[:, :, :, 64:65], 1.0), writes=[buf("vaug_all")])
                    tk.op(P_, lambda e: e.memset(vaug[:, :, :, 66:67], 1.0), writes=[buf("vaug_all")])
                    for i in range(2):
                        tk.op(P_, lambda e, i=i: e.memset(rd[i][:, :], 1.0), writes=[buf("rd%d" % i)])
                        tk.op(P_, lambda e, i=i: e.memset(rdr[i][:, :], 1.0), writes=[buf("rdr%d" % i)])

                    wb, wbB = wgv
                    for t in range(NT):
                        a, aB = inproj_tok(t, wb, wbB, 512)
                        av = a[:, :].rearrange("p (g h d) -> p g h d", g=4, h=2)
                        tk.op(S_, lambda e, t=t, av=av: e.copy(out=vaug[:, t, :, 0:64], in_=av[:, :, 0, :]),
                              reads=[aB, buf("vaug_all")], writes=[buf("vaug%d" % t)])
                        tk.op(V, lambda e, t=t, av=av: e.tensor_copy(out=vaug[:, t, :, 130:194], in_=av[:, :, 1, :]),
                              reads=[aB, buf("vaug_all")], writes=[buf("vaug%d" % t)])

                    def pair_chunks(p):
                        pb_ = p % 2
                        qT, kT, sgT = qTs[pb_], kTs[pb_], sgTs[pb_]
                        holder = {}
                        units = []

                        def u_load():
                            wb, wbB = next_w()
                            holder["w"] = (wb, wbB)
                            for j, c0 in enumerate([p * 128, 512 + p * 128, 1536 + p * 128]):
                                tk.dma(P_, wb[:, :, j * 128:(j + 1) * 128],
                                       win_d[:, c0:c0 + 128].rearrange("(k p) n -> p k n", p=128), writes=[wbB[j]])
                        units.append(u_load)

                        def mk_q(c):
                            def u():
                                wb, wbB = holder["w"]
                                a, aB = inproj_feat(c, wb, wbB, 0)
                                tk.op(S_, lambda e: e.activation(out=qT[:, c * 512:(c + 1) * 512], in_=a[:, :],
                                                                 func=AF.Copy, scale=0.125),
                                      reads=[aB], writes=[buf("qT%d_%d" % (pb_, c))])
                            return u

                        def mk_k(c):
                            def u():
                                wb, wbB = holder["w"]
                                ntok = 512 if c < 4 else 256
                                a, aB = inproj_feat(c, wb, wbB, 128, ntok)
                                if c % 2 == 0:
                                    tk.op(S_, lambda e: e.copy(out=kT[:, c * 512:c * 512 + ntok], in_=a[:, 0:ntok]),
                                          reads=[aB], writes=[buf("kT%d_%d" % (pb_, c))])
                                else:
                                    tk.op(V, lambda e: e.tensor_copy(out=kT[:, c * 512:c * 512 + ntok], in_=a[:, 0:ntok]),
                                          reads=[aB], writes=[buf("kT%d_%d" % (pb_, c))])
                            return u

                        def mk_g(c):
                            def u():
                                wb, wbB = holder["w"]
                                a, aB = inproj_feat(c, wb, wbB, 256)
                                th, thB = ths[c % 2], buf("th%d" % (c % 2))
                                tk.op(S_, lambda e: e.activation(out=th[:, :], in_=a[:, :], func=AF.Tanh, scale=0.5),
                                      reads=[aB], writes=[thB])
                                tk.op(V, lambda e: e.scalar_tensor_tensor(out=sgT[:, c * 512:(c + 1) * 512], in0=th[:, :],
                                                                          scalar=1.0, in1=a[:, :], op0=ALU.add, op1=ALU.mult),
                                      reads=[thB, aB], writes=[buf("sgT%d_%d" % (pb_, c))])
                            return u
                        for c in range(5):
                            units.append(mk_k(c))
                        for c in range(4):
                            units.append(mk_q(c))
                            units.append(mk_g(c))
                        return units

                    gctr = [0]

                    def attention(p, inject):
                        pb_ = p % 2
                        qT, kT, sgT = qTs[pb_], kTs[pb_], sgTs[pb_]
                        G = []
                        for i in range(8):
                            if i == 0:
                                Rs = [0, 2, 4, 6]
                            elif i == 7:
                                Rs = [24, 26, 28, 30]
                            else:
                                Rs = [4 * i - 4 + 2 * t for t in range(6)]
                            if i in (0, 7):
                                groups = [[(Rs[1], 0, 4, 0), (Rs[0], 0, 4, 256)], [(Rs[3], 0, 4, 0), (Rs[2], 0, 4, 256)]]
                            else:
                                groups = [[(Rs[1], 0, 4, 0), (Rs[0], 0, 2, 256)],
                                          [(Rs[3], 0, 4, 0), (Rs[2], 0, 4, 256)],
                                          [(Rs[5], 3, 4, 0), (Rs[4], 1, 4, 64)]]
                            groups = groups + [None]
                            nmm_head = sum(2 for _ in groups)
                            for hh in range(2):
                                mm0 = 0
                                for gi, grp in enumerate(groups):
                                    G.append(dict(i=i, hh=hh, grp=grp, gi=gi, ng=len(groups), mm0=mm0, nmm=nmm_head,
                                                  last=(hh == 1 and gi == len(groups) - 1)))
                                    mm0 += 2
                        for g in G:
                            c_ = gctr[0]
                            gctr[0] += 1
                            g["sc"] = sc3[c_ % NSC]
                            g["eS"] = (eS[c_ % NSC], buf("eS%d" % (c_ % NSC)))
                            g["PT"] = (PT[c_ % (NSC + 1)], buf("PT%d" % (c_ % (NSC + 1))))
                            g["PTb"] = buf("PTb%d" % (c_ % (NSC + 1)))

                        def front(g):
                            i, hh, grp = g["i"], g["hh"], g["grp"]
                            hs = slice(hh * 64, hh * 64 + 64)
                            hidx = 2 * p + hh
                            Mt = Mtab[0] if i in (0, 7) else Mtab[1]
                            s_t, sB_ = g["sc"]
                            e_t, eB = g["eS"]
                            p_t, pB = g["PT"]
                            qB = buf("qT%d_%d" % (pb_, i // 2))
                            if grp is None:
                                tiles = [(2048, 0, 4, 0), (2176, 0, 4, 256)]
                            else:
                                tiles = [(R * 64, qlo, qhi, pos) for (R, qlo, qhi, pos) in grp]
                            g["tiles"] = tiles
                            kbufs = [buf("kT%d_%d" % (pb_, kt // 512)) for (kt, _, _, _) in tiles]
                            ncol = max(pos + (qhi - qlo) * 64 for (_, qlo, qhi, pos) in tiles)
                            e_f = e_t[:, :, :].rearrange("p a q -> p (a q)")
                            p_f = p_t[:, :, :].rearrange("p a q -> p (a q)")
                            for j, (kt, qlo, qhi, pos) in enumerate(tiles):
                                nq = (qhi - qlo) * 64
                                tk.op(T_, lambda e, kt=kt, qlo=qlo, nq=nq, pos=pos: e.matmul(
                                    s_t[:, pos:pos + nq], lhsT=kT[hs, kt:kt + 128],
                                    rhs=qT[hs, i * 256 + qlo * 64:i * 256 + qlo * 64 + nq], start=True, stop=True),
                                    reads=kbufs + [qB], writes=[sB_], inc=(j == 1))
                            if grp is None:
                                tk.op(S_, lambda e: e.activation(out=p_f[:, 0:ncol], in_=s_t[:, 0:ncol], func=AF.Exp),
                                      reads=[sB_], writes=[pB, g["PTb"]])
                            else:
                                tk.op(S_, lambda e: e.activation(out=e_f[:, 0:ncol], in_=s_t[:, 0:ncol], func=AF.Exp),
                                      reads=[sB_], writes=[eB])
                                for j, (kt, qlo, qhi, pos) in enumerate(tiles):
                                    nq = (qhi - qlo) * 64
                                    s0 = 7 - (kt // 64 - 4 * i)
                                    tk.op(V, lambda e, qlo=qlo, qhi=qhi, nq=nq, pos=pos, s0=s0: e.tensor_tensor(
                                        out=p_f[:, pos:pos + nq], in0=e_f[:, pos:pos + nq],
                                        in1=Mt[:, hidx, s0 + qlo:s0 + qhi, :].rearrange("p s q -> p (s q)"), op=ALU.mult),
                                        reads=[eB, buf("Mtab")], writes=[pB if j == 0 else g["PTb"]])

                        def back(g):
                            i, hh = g["i"], g["hh"]
                            o_t, oB = ob[i % 2]
                            p_t, pB = g["PT"]
                            p_f = p_t[:, :, :].rearrange("p a q -> p (a q)")
                            nmm = g["nmm"]
                            for j, (kt, qlo, qhi, pos) in enumerate(g["tiles"]):
                                nq = (qhi - qlo) * 64
                                tile = kt // 128
                                imm = g["mm0"] + j
                                if hh == 0:
                                    o_dst = o_t[0:65, qlo * 64:qlo * 64 + nq]
                                    vcols = slice(0, 65)
                                else:
                                    o_dst = o_t[:, 256 + qlo * 64:256 + qlo * 64 + nq]
                                    vcols = slice(66, 194)
                                tk.op(T_, lambda e, tile=tile, imm=imm, o_dst=o_dst, vcols=vcols, pos=pos, nq=nq: e.matmul(
                                    o_dst, lhsT=vaug[:, tile, p, vcols], rhs=p_f[:, pos:pos + nq],
                                    start=(imm == 0), stop=(imm == nmm - 1)),
                                    reads=[pB, g["PTb"], buf("vaug%d" % tile)], writes=[oB], inc=(imm == nmm - 1))

                        def fin1(i):
                            o_t, oB = ob[i % 2]
                            r_r, r_rB = rdr[i % 2], buf("rdr%d" % (i % 2))
                            r_d, r_dB = rd[i % 2], buf("rd%d" % (i % 2))
                            tk.op(S_, lambda e: e.copy(out=r_r[64:65, :], in_=o_t[64:65, 0:256]), reads=[oB], writes=[r_rB])
                            tk.op(S_, lambda e: e.copy(out=r_r[0:1, :], in_=o_t[0:1, 256:512]), reads=[oB], writes=[r_rB])
                            tk.op(V, lambda e: e.reciprocal(out=r_d[0:65, :], in_=r_r[0:65, :]),
                                  reads=[r_rB], writes=[r_dB])

                        def fin2(i):
                            o_t, oB = ob[i % 2]
                            r_d, r_dB = rd[i % 2], buf("rd%d" % (i % 2))
                            tt, ttB = tts[i % 2], buf("tt%d" % (i % 2))
                            tk.op(T_, lambda e: e.matmul(bcb[:, 0:256], lhsT=oeo[:, :], rhs=r_d[:, :], start=True, stop=True),
                                  reads=[r_dB, buf("oeo")], writes=[bcB])
                            tk.op(V, lambda e: e.tensor_tensor(out=tt[:, :], in0=sgT[:, i * 256:(i + 1) * 256],
                                                               in1=bcb[:, 0:256], op=ALU.mult),
                                  reads=[buf("sgT%d_%d" % (pb_, i // 2)), bcB], writes=[ttB])
                            tk.op(V, lambda e: e.tensor_tensor(out=yT[0:64, p, i * 256:(i + 1) * 256],
                                                               in0=o_t[0:64, 0:256], in1=tt[0:64, :], op=ALU.mult),
                                  reads=[oB, ttB], writes=[buf("yT%d_%d" % (p, i))])
                            tk.op(V, lambda e: e.tensor_tensor(out=yT[64:128, p, i * 256:(i + 1) * 256],
                                                               in0=o_t[64:128, 256:512], in1=tt[64:128, :], op=ALU.mult),
                                  reads=[oB, ttB], writes=[buf("yT%d_%d" % (p, i))])

                        n = len(G)
                        pend = []
                        inj = list(inject)
                        stride = max(1, n // (len(inj) + 1)) if inj else n + 1
                        SK = NSC - 1
                        for k_ in range(SK):
                            front(G[k_])
                        for idx in range(n):
                            if idx + SK < n:
                                front(G[idx + SK])
                            back(G[idx])
                            if G[idx]["last"]:
                                fin1(G[idx]["i"])
                                pend.append((idx + FIN_DELAY, G[idx]["i"]))
                            while pend and pend[0][0] <= idx:
                                fin2(pend.pop(0)[1])
                            if inj and idx % stride == stride - 1:
                                inj.pop(0)()
                        for _, i in pend:
                            fin2(i)
                        for u in inj:
                            u()

                    if stop == "NA2":
                        pass
                    units = pair_chunks(0)
                    for u in units:
                        u()
                    acc_only0[0] = (NSC == 4)
                    for p in range(4 if stop != "NA1" else 0):
                        nxt = pair_chunks(p + 1) if p < 3 else []
                        if p == 3:
                            wbr, wbrB = next_w()
                            for j, c0 in enumerate([2048, 2560, 3072, 3584]):
                                tk.dma(P_, wbr[:, :, j * 128:(j + 1) * 128],
                                       win_d[:, c0:c0 + 128].rearrange("(k p) n -> p k n", p=128), writes=[wbrB[j]])
                            ret_w_pre[0] = (wbr, wbrB)
                        if INTERLEAVE_NA:
                            attention(p, nxt)
                        else:
                            attention(p, [])
                            for u in nxt:
                                u()
                    barrier()
                with ExitStack() as sR:
                    nheads_ret = 0 if stop in ("NA", "NA1", "NA2") else 4
                    qks = [sbt(sR, "qk%d" % i, [128, 16, 2, 128], BF16) for i in range(2)]
                    vts = [sbt(sR, "vt%d" % i, [128, 16, 128], BF16) for i in range(2)]
                    sgs = [sbt(sR, "sg%d" % i, [128, 16, 128], BF16) for i in range(2)]
                    kcxs = [sbt(sR, "kcx%d" % i, [128, 2, 128], BF16) for i in range(2)]
                    vcfs = [sbt(sR, "vcf%d" % i, [128, 2, 128], BF16) for i in range(2)]
                    vcbs = [sbt(sR, "vcb%d" % i, [128, 2, 128], BF16) for i in range(2)]
                    Sb16s = [sbt(sR, "Sb16_%d" % i, [128, 16, 128], BF16) for i in range(2)]
                    Sf32s = [sbt(sR, "Sf32_%d" % i, [128, 128], F32) for i in range(2)]
                    Sb32s = [sbt(sR, "Sb32_%d" % i, [128, 128], F32) for i in range(2)]
                    Sf16 = [sbt(sR, "Sf16_%d" % i, [128, 128], BF16) for i in range(5)]
                    T1s = [sbt(sR, "T1_%d" % i, [128, 2, 2, 64], F32) for i in range(2)]
                    Us = [sbt(sR, "U_%d" % i, [128, 2, 2, 64], F32) for i in range(2)]
                    th2s = [sbt(sR, "th2_%d" % i, [128, 128], F32) for i in range(2)]
                    junkR = sbt(sR, "junkR", [128, 128], F32)
                    ssR = sbt(sR, "ssR", [128, 4, 4], F32)
                    NTB = 4
                    tmpb = {nm: [sbt(sR, "%s%d" % (nm, i), [128, 128], BF16) for i in range(NTB)]
                            for nm in ("qT_", "kT_", "qfT", "qbT", "AT", "vwf", "vwb", "yt")}

                    def tb(nm, n):
                        return tmpb[nm][n % NTB], buf("%s%d" % (nm, n % NTB))

                    def pass1_units(h):
                        hb_ = h % 2
                        qk, vt, sg, kcx, vcf, vcb = qks[hb_], vts[hb_], sgs[hb_], kcxs[hb_], vcfs[hb_], vcbs[hb_]
                        holder = {}
                        units = []

                        def u_load():
                            if h in ret_w_pre:
                                holder["w"] = ret_w_pre.pop(h)
                                return
                            wb, wbB = next_w()
                            holder["w"] = (wb, wbB)
                            for j, c0 in enumerate([2048, 2560, 3072, 3584]):
                                tk.dma(P_, wb[:, :, j * 128:(j + 1) * 128],
                                       win_d[:, c0 + h * 128:c0 + (h + 1) * 128].rearrange("(k p) n -> p k n", p=128),
                                       writes=[wbB[j]])
                        units.append(u_load)

                        def mk_tile(t):
                            def u():
                                wb, wbB = holder["w"]
                                a, aB = inproj_tok(t, wb, wbB, 512)
                                T1, T1B = T1s[t % 2], buf("T1_%d" % (t % 2))
                                U, U0B, U1B = Us[t % 2], buf("U0_%d" % (t % 2)), buf("U1_%d" % (t % 2))
                                th2, th2B = th2s[t % 2], buf("th2_%d" % (t % 2))
                                av = a[:, 0:256].rearrange("p (a h d) -> p a h d", a=2, h=2)
                                cosb = cossin[:, 0, t:t + 1, :].unsqueeze(1).to_broadcast([128, 2, 2, 64])
                                sinb = cossin[:, 1, t:t + 1, :].to_broadcast([128, 2, 64])
                                tk.op(V, lambda e: e.tensor_tensor(out=T1[:, :, :, :], in0=av, in1=cosb, op=ALU.mult),
                                      reads=[aB, buf("cossin")], writes=[T1B])
                                tk.op(V, lambda e: e.tensor_tensor(out=U[:, :, 0, :], in0=av[:, :, 1, :], in1=sinb, op=ALU.mult),
                                      reads=[aB, buf("cossin")], writes=[U0B])
                                tk.op(V, lambda e: e.tensor_tensor(out=U[:, :, 1, :], in0=av[:, :, 0, :], in1=sinb, op=ALU.mult),
                                      reads=[aB, buf("cossin")], writes=[U1B])
                                tk.op(P_, lambda e: e.tensor_tensor(out=qk[:, t, :, 0:64], in0=T1[:, :, 0, :], in1=U[:, :, 0, :],
                                                                    op=ALU.subtract),
                                      reads=[T1B, U0B], writes=[buf("qk%d_%d" % (hb_, t))])
                                tk.op(P_, lambda e: e.tensor_tensor(out=qk[:, t, :, 64:128], in0=T1[:, :, 1, :], in1=U[:, :, 1, :],
                                                                    op=ALU.add),
                                      reads=[T1B, U1B], writes=[buf("qk%d_%d" % (hb_, t))])
                                tk.op(S_, lambda e: e.copy(out=vt[:, t, :], in_=a[:, 256:384]),
                                      reads=[aB], writes=[buf("vt%d_%d" % (hb_, t))])
                                if RET_SILU:
                                    tk.op(S_, lambda e: e.activation(out=sg[:, t, :], in_=a[:, 384:512], func=AF.Silu),
                                          reads=[aB], writes=[buf("sg%d_%d" % (hb_, t))])
                                else:
                                    tk.op(S_, lambda e: e.activation(out=th2[:, :], in_=a[:, 384:512], func=AF.Tanh, scale=0.5),
                                          reads=[aB], writes=[th2B])
                                    tk.op(V, lambda e: e.scalar_tensor_tensor(out=sg[:, t, :], in0=th2[:, :], scalar=1.0,
                                                                              in1=a[:, 384:512], op0=ALU.add, op1=ALU.mult),
                                          reads=[th2B, aB], writes=[buf("sg%d_%d" % (hb_, t))])
                            return u

                        def mk_ctx(t):
                            def u():
                                wb, wbB = holder["w"]
                                a, aB = inproj_tok(t, wb, wbB, 256, c0=128)
                                ci = t - 16
                                cf_col = rcols[:, h, 2:3] if ci == 0 else rcols[:, h, 0:1]
                                cb_col = rcols[:, h, 1:2] if ci == 0 else rcols[:, h, 3:4]
                                tk.op(S_, lambda e: e.copy(out=kcx[:, ci, :], in_=a[:, 0:128]),
                                      reads=[aB], writes=[buf("kcx%d" % hb_)])
                                tk.op(V, lambda e: e.tensor_scalar(out=vcf[:, ci, :], in0=a[:, 128:256], scalar1=cf_col,
                                                                   scalar2=None, op0=ALU.mult),
                                      reads=[aB, buf("rcols")], writes=[buf("vcf%d" % hb_)])
                                tk.op(V, lambda e: e.tensor_scalar(out=vcb[:, ci, :], in0=a[:, 128:256], scalar1=cb_col,
                                                                   scalar2=None, op0=ALU.mult),
                                      reads=[aB, buf("rcols")], writes=[buf("vcb%d" % hb_)])
                            return u

                        def u_state():
                            s_t, sB_ = pbank[3]
                            for ci in range(2):
                                tk.op(T_, lambda e, ci=ci: e.matmul(s_t[:, 0:128], lhsT=kcx[:, ci, :], rhs=vcf[:, ci, :],
                                                                    start=(ci == 0), stop=(ci == 1)),
                                      reads=[buf("kcx%d" % hb_), buf("vcf%d" % hb_)], writes=[sB_], inc=(ci == 1))
                            for ci in range(2):
                                tk.op(T_, lambda e, ci=ci: e.matmul(s_t[:, 128:256], lhsT=kcx[:, ci, :], rhs=vcb[:, ci, :],
                                                                    start=(ci == 0), stop=(ci == 1)),
                                      reads=[buf("kcx%d" % hb_), buf("vcb%d" % hb_)], writes=[sB_], inc=(ci == 1))
                            tk.op(V, lambda e: e.tensor_copy(out=Sf32s[hb_][:, :], in_=s_t[:, 0:128]),
                                  reads=[sB_], writes=[buf("Sf32_%d" % hb_)])
                            tk.op(V, lambda e: e.tensor_copy(out=Sb32s[hb_][:, :], in_=s_t[:, 128:256]),
                                  reads=[sB_], writes=[buf("Sb32_%d" % hb_)])
                        units.append(mk_ctx(16))
                        units.append(mk_ctx(17))
                        units.append(u_state)
                        for t in range(16):
                            units.append(mk_tile(t))
                        return units

                    def sweep_units(h, bwd_banks=(2, 3)):
                        hb_ = h % 2
                        qk, vt, sg, Sb16 = qks[hb_], vts[hb_], sgs[hb_], Sb16s[hb_]
                        Sf32, Sb32 = Sf32s[hb_], Sb32s[hb_]
                        Sf32B, Sb32B = buf("Sf32_%d" % hb_), buf("Sb32_%d" % hb_)
                        qkB = lambda n: buf("qk%d_%d" % (hb_, n))
                        vtB = lambda n: buf("vt%d_%d" % (hb_, n))
                        units = []

                        def B1(n):
                            if n <= 0:
                                return
                            vwb, vwbB = tb("vwb", n)
                            tk.op(S_, lambda e: e.activation(out=vwb[:, :], in_=vt[:, n, :], func=AF.Copy,
                                                             scale=rcols[:, h, 1:2]),
                                  reads=[vtB(n), buf("rcols")], writes=[vwbB])

                        def B2(n):
                            tk.op(S_, lambda e: e.copy(out=Sb16[:, n, :], in_=Sb32[:, :]),
                                  reads=[Sb32B], writes=[buf("Sb16_%d_%d" % (hb_, n))])
                            if n == 0:
                                return
                            vwb, vwbB = tb("vwb", n)
                            s_t, sB_ = pbank[bwd_banks[n % 2]]
                            tk.op(T_, lambda e: e.matmul(s_t[:, 256:384], lhsT=qk[:, n, 1, :], rhs=vwb[:, :],
                                                         start=True, stop=True),
                                  reads=[qkB(n), vwbB], writes=[sB_])
                            tk.op(V, lambda e: e.scalar_tensor_tensor(out=Sb32[:, :], in0=Sb32[:, :], scalar=rcols[:, h, 5:6],
                                                                      in1=s_t[:, 256:384], op0=ALU.mult, op1=ALU.add),
                                  reads=[Sb32B, sB_, buf("rcols")], writes=[Sb32B])

                        def mk_bwd(n):
                            def u():
                                if n == 15:
                                    B1(15)
                                    B1(14)
                                B1(n - 2)
                                B2(n)
                            return u
                        for n in range(15, -1, -1):
                            units.append(mk_bwd(n))
                        bwd_list = units
                        units = []

                        NSF = 5

                        def F0():
                            tk.op(S_, lambda e: e.copy(out=Sf16[0][:, :], in_=Sf32[:, :]),
                                  reads=[Sf32B], writes=[buf("Sf16_0")])

                        def U1(n):
                            if n >= 15:
                                return
                            vwf, vwfB = tb("vwf", n)
                            tk.op(S_, lambda e: e.activation(out=vwf[:, :], in_=vt[:, n, :], func=AF.Copy, scale=rcols[:, h, 0:1]),
                                  reads=[vtB(n), buf("rcols")], writes=[vwfB])

                        def F1a(n):
                            tp_, tpB_ = tp2[0]
                            for j in range(2):
                                tk.op(T_, lambda e, j=j: e.transpose(out=tp_[:, j, :], in_=qk[:, n, j, :], identity=ident[:, :]),
                                      reads=[qkB(n), buf("ident")], writes=[tpB_], inc=(j == 1))
                            qT_, qTB = tb("qT_", n)
                            kT_, kTB = tb("kT_", n)
                            qfT, qfB = tb("qfT", n)
                            qbT, qbB = tb("qbT", n)
                            tk.op(S_, lambda e: e.copy(out=qT_[:, :], in_=tp_[:, 0, :]), reads=[tpB_], writes=[qTB])
                            tk.op(V, lambda e: e.tensor_copy(out=kT_[:, :], in_=tp_[:, 1, :]), reads=[tpB_], writes=[kTB])
                            tk.op(P_, lambda e: e.tensor_tensor(out=qfT[:, :], in0=qT_[:, :], in1=tabq[:, h, 0, :], op=ALU.mult),
                                  reads=[qTB, buf("tabq")], writes=[qfB])
                            tk.op(P_, lambda e: e.tensor_tensor(out=qbT[:, :], in0=qT_[:, :], in1=tabq[:, h, 1, :], op=ALU.mult),
                                  reads=[qTB, buf("tabq")], writes=[qbB])

                        def F1b(n):
                            qT_, qTB = tb("qT_", n)
                            kT_, kTB = tb("kT_", n)
                            AT, ATB = tb("AT", n)
                            s_t, sB_ = pbank[2]
                            tk.op(T_, lambda e: e.matmul(s_t[:, 0:128], lhsT=kT_[:, :], rhs=qT_[:, :], start=True, stop=True),
                                  reads=[kTB, qTB], writes=[sB_])
                            tk.op(V, lambda e: e.tensor_tensor(out=AT[:, :], in0=s_t[:, 0:128], in1=DT[:, h, :], op=ALU.mult),
                                  reads=[sB_, buf("DT")], writes=[ATB])

                        def U_(n):
                            if n >= 15:
                                return
                            vwf, vwfB = tb("vwf", n)
                            s_t, sB_ = pbank[3]
                            tk.op(T_, lambda e: e.matmul(s_t[:, 0:128], lhsT=qk[:, n, 1, :], rhs=vwf[:, :], start=True, stop=True),
                                  reads=[qkB(n), vwfB], writes=[sB_])
                            tk.op(V, lambda e: e.scalar_tensor_tensor(out=Sf32[:, :], in0=Sf32[:, :], scalar=rcols[:, h, 4:5],
                                                                      in1=s_t[:, 0:128], op0=ALU.mult, op1=ALU.add),
                                  reads=[Sf32B, sB_, buf("rcols")], writes=[Sf32B])
                            tk.op(S_, lambda e: e.copy(out=Sf16[(n + 1) % NSF][:, :], in_=Sf32[:, :]),
                                  reads=[Sf32B], writes=[buf("Sf16_%d" % ((n + 1) % NSF))])

                        def F2a(n):
                            qfT, qfB = tb("qfT", n)
                            qbT, qbB = tb("qbT", n)
                            AT, ATB = tb("AT", n)
                            o_t, oB = ob[n % 2]
                            sf16, sf16B = Sf16[n % NSF], buf("Sf16_%d" % (n % NSF))
                            c = n % 4
                            sB_ = buf("ssR%d" % c)
                            tk.op(T_, lambda e: e.matmul(o_t[:, 0:128], lhsT=AT[:, :], rhs=vt[:, n, :], start=True, stop=False),
                                  reads=[ATB, vtB(n)], writes=[oB], inc=False)
                            tk.op(T_, lambda e: e.matmul(o_t[:, 0:128], lhsT=qfT[:, :], rhs=sf16[:, :], start=False, stop=False),
                                  reads=[qfB, sf16B], writes=[oB], inc=False)
                            tk.op(T_, lambda e: e.matmul(o_t[:, 0:128], lhsT=qbT[:, :], rhs=Sb16[:, n, :], start=False, stop=True),
                                  reads=[qbB, buf("Sb16_%d_%d" % (hb_, n))], writes=[oB])
                            tk.op(S_, lambda e: e.activation(out=junkR[:, :], in_=o_t[:, 0:128], func=AF.Square,
                                                             accum_out=ssR[:, c, 0:1]),
                                  reads=[oB], writes=[buf("junkR"), sB_])
                            tk.op(V, lambda e: e.tensor_scalar(out=ssR[:, c, 1:2], in0=ssR[:, c, 0:1], scalar1=1.0 / 128,
                                                               scalar2=EPS, op0=ALU.mult, op1=ALU.add),
                                  reads=[sB_], writes=[sB_])
                            tk.op(P_, lambda e: e.tensor_tensor(out=ssR[:, c, 2:3], in0=ssR[:, c, 1:2], in1=cneg[:, 0:1], op=ALU.pow),
                                  reads=[sB_, buf("cneg")], writes=[sB_])

                        def F2b(n):
                            yt, ytB = tb("yt", n)
                            o_t, oB = ob[n % 2]
                            c = n % 4
                            sB_ = buf("ssR%d" % c)
                            tk.op(V, lambda e: e.scalar_tensor_tensor(out=yt[:, :], in0=o_t[:, 0:128], scalar=ssR[:, c, 2:3],
                                                                      in1=sg[:, n, :], op0=ALU.mult, op1=ALU.mult),
                                  reads=[oB, sB_, buf("sg%d_%d" % (hb_, n))], writes=[ytB])

                        def F3(n):
                            yt, ytB = tb("yt", n)
                            tp_, tpB_ = tp2[1]
                            tk.op(T_, lambda e: e.transpose(out=tp_[:, 0, :], in_=yt[:, :], identity=ident[:, :]),
                                  reads=[ytB, buf("ident")], writes=[tpB_])
                            tk.op(S_, lambda e: e.copy(out=yT[:, 4 + h, n * 128:(n + 1) * 128], in_=tp_[:, 0, :]),
                                  reads=[tpB_], writes=[buf("yTr%d_%d" % (h, n))])

                        def mk_fwd(s_):
                            def u():
                                if s_ == 0:
                                    F0()
                                    U1(0)
                                if s_ < 16:
                                    U1(s_ + 1)
                                    F1a(s_)
                                    U_(s_)
                                if 0 <= s_ - 1 < 16:
                                    F1b(s_ - 1)
                                if 0 <= s_ - 2 < 16:
                                    F2a(s_ - 2)
                                if 0 <= s_ - 3 < 16:
                                    F2b(s_ - 3)
                                if 0 <= s_ - 4 < 16:
                                    F3(s_ - 4)
                            return u
                        for s_ in range(20):
                            units.append(mk_fwd(s_))
                        return bwd_list, units

                    wsts = [sbt(sR, "wst%d" % i, [128, D], F32) for i in range(2)]
                    rngh = sbt(sR, "rngh", [128, 4], F32)

                    def wo_view(k, c0, c1):
                        return hT[:, k // 2, (k % 2) * 1024 + c0:(k % 2) * 1024 + c1]

                    def wout_units():
                        units = []

                        def u0():
                            tk.op(V, lambda e: e.tensor_scalar(out=rngh[:, :], in0=rngc[:, :], scalar1=0.5, scalar2=None,
                                                               op0=ALU.mult),
                                  reads=[buf("rngc")], writes=[buf("rngh")])
                        units.append(u0)

                        def mk(k):
                            def u():
                                w_s, w_sB = wsts[k % 2], buf("wst%d" % (k % 2))
                                tk.dma(Y_, w_s[:, :], wout_d[k * 128:(k + 1) * 128, :], writes=[w_sB])
                                sc1 = 0.5 if k < 4 else (rngc[:, k - 4:k - 3] if RET_SILU else rngh[:, k - 4:k - 3])
                                tk.op(V, lambda e: e.tensor_scalar(out=wo_view(k, 0, 1024), in0=w_s[:, :], scalar1=sc1,
                                                                   scalar2=None, op0=ALU.mult),
                                      reads=[w_sB, buf("rngh")], writes=hT_all[0:16] + [buf("wo")])
                            return u
                        for k in range(8):
                            units.append(mk(k))
                        return units

                    def interleave(*lists):
                        lists = [list(l) for l in lists if l]
                        if not lists:
                            return
                        nmax = max(len(l) for l in lists)
                        pos = [0] * len(lists)
                        for step in range(nmax):
                            for li, l in enumerate(lists):
                                tgt = ((step + 1) * len(l) + nmax - 1) // nmax
                                while pos[li] < min(tgt, len(l)):
                                    l[pos[li]]()
                                    pos[li] += 1

                    def prefetch_ret_w(h):
                        wbr, wbrB = next_w()
                        for j, c0 in enumerate([2048, 2560, 3072, 3584]):
                            tk.dma(P_, wbr[:, :, j * 128:(j + 1) * 128],
                                   win_d[:, c0 + h * 128:c0 + (h + 1) * 128].rearrange("(k p) n -> p k n", p=128),
                                   writes=[wbrB[j]])
                        ret_w_pre[h] = (wbr, wbrB)

                    if nheads_ret == 4:
                        if RET_SCHED == 0:
                            interleave(pass1_units(0))
                            for h in range(4):
                                bw, fw = sweep_units(h)
                                pu = pass1_units(h + 1) if h + 1 < 4 else wout_units()
                                interleave(bw + fw, pu)
                        else:
                            prefetch_ret_w(1)
                            interleave(pass1_units(0))
                            bw0, fw0 = sweep_units(0)
                            interleave(bw0, pass1_units(1))
                            bw1, fw1 = sweep_units(1, bwd_banks=(0, 1))
                            prefetch_ret_w(2)
                            interleave(fw0, bw1)
                            p1_2 = pass1_units(2)
                            prefetch_ret_w(3)
                            interleave(fw1, p1_2)
                            bw2, fw2 = sweep_units(2)
                            interleave(bw2, pass1_units(3))
                            bw3, fw3 = sweep_units(3, bwd_banks=(0, 1))
                            interleave(fw2, bw3)
                            interleave(fw3, wout_units())
                    barrier()
            if stop in ("NA", "RET", "NA1", "NA2"):
                if "yT" in dbg and b == 0:
                    with ExitStack() as sd:
                        dbg32 = sbt(sd, "dbgy", [128, 2048], F32)
                        for k in range(4 if stop == "NA" else 8):
                            tk.op(V, lambda e, k=k: e.tensor_copy(out=dbg32[:, :], in_=yT[:, k, :]), writes=[buf("dbgy")])
                            tk.dma(Y_, dbg_out["yT"][:, k * L:(k + 1) * L], dbg32[:, :], reads=[buf("dbgy")],
                                   writes=[buf("dbg_yT")])
                        barrier()
                return
            with ExitStack() as sD:
                xs2 = [sbt(sD, "xsD%d" % i, [128, D], F32) for i in range(3)]
                ost = [sbt(sD, "ost%d" % i, [128, D], F32) for i in range(2)]
                r32s = [sbt(sD, "r32_%d" % i, [128, D], F32) for i in range(4)]
                junk2 = sbt(sD, "junkD", [128, D], BF16)
                gate_t = sbt(sD, "gate_t", [128, D], F32)
                fng_t = sbt(sD, "fng_t", [128, D], F32)
                tk.dma(Y_, gate_t[:, :], mod_scr[b:b + 1, 2 * D:3 * D].to_broadcast([128, D]),
                       reads=[buf("mod_scr_g")], writes=[buf("gate_t")])
                tk.dma(Y_, fng_t[:, :], fng_d[0:1, :].to_broadcast([128, D]), writes=[buf("fng_t")])
                yT_all = [bb for nm, bb in B.items() if nm.startswith("yT")]
                xs3 = xs2
                if "wo" in dbg and b == 0:
                    for k in range(8):
                        tk.op(V, lambda e, k=k: e.tensor_copy(out=r32s[0][:, :], in_=wo_view(k, 0, 1024)),
                              reads=[buf("wo")] + hT_all[0:16], writes=[buf("r32_0")])
                        tk.dma(Y_, dbg_out["wo"][:, k * D:(k + 1) * D], r32s[0][:, :], reads=[buf("r32_0")],
                               writes=[buf("dbg_wo")])
                ssD = sbt(sD, "ssD", [128, 4, 4], F32)

                def D_s0(t):
                    xsb, xB = xs3[t % 3], buf("xsD%d" % (t % 3))
                    tk.dma(Y_, xsb[:, :], x_d[b, t * 128:(t + 1) * 128, :], writes=[xB])
                    for half in range(2):
                        a, aB = pbank[(t % 2) * 2 + half]
                        for k in range(8):
                            tk.op(T_, lambda e, k=k, a=a, half=half: e.matmul(
                                a[:, :], lhsT=yT[:, k, t * 128:(t + 1) * 128], rhs=wo_view(k, half * 512, (half + 1) * 512),
                                start=(k == 0), stop=(k == 7)),
                                reads=yT_all + [buf("wo")] + hT_all[0:16], writes=[aB], inc=(k == 7))

                NR = 4

                def rbuf(t):
                    return r32s[t % NR], buf("r32_%d" % (t % NR)), buf("r32h_%d" % (t % NR))

                def D_s1a(t):
                    r, rB, rhB = rbuf(t)
                    for half in range(2):
                        a, aB = pbank[(t % 2) * 2 + half]
                        tk.op(V, lambda e, a=a, half=half: e.tensor_tensor(out=r[:, half * 512:(half + 1) * 512], in0=a[:, :],
                                                                           in1=gate_t[:, half * 512:(half + 1) * 512],
                                                                           op=ALU.mult),
                              reads=[aB, buf("gate_t")], writes=[rB if half == 0 else rhB])

                def D_s1b(t):
                    xsb, xB = xs3[t % 3], buf("xsD%d" % (t % 3))
                    r, rB, rhB = rbuf(t)
                    c = t % 4
                    tk.op(P_, lambda e: e.tensor_tensor(out=r[:, 0:512], in0=r[:, 0:512], in1=xsb[:, 0:512], op=ALU.add),
                          reads=[rB, xB], writes=[rB])
                    tk.op(V, lambda e: e.tensor_tensor(out=r[:, 512:1024], in0=r[:, 512:1024], in1=xsb[:, 512:1024], op=ALU.add),
                          reads=[rhB, xB], writes=[rhB])
                    tk.op(S_, lambda e: e.activation(out=junk2[:, :], in_=r[:, :], func=AF.Square, accum_out=ssD[:, c, 0:1]),
                          reads=[rB, rhB], writes=[buf("junkD"), buf("ssD%d" % c)])

                def D_s2a(t):
                    c = t % 4
                    sB_ = buf("ssD%d" % c)
                    tk.op(V, lambda e: e.tensor_scalar(out=ssD[:, c, 1:2], in0=ssD[:, c, 0:1], scalar1=1.0 / D, scalar2=EPS,
                                                       op0=ALU.mult, op1=ALU.add),
                          reads=[sB_], writes=[sB_])
                    tk.op(P_, lambda e: e.tensor_tensor(out=ssD[:, c, 2:3], in0=ssD[:, c, 1:2], in1=cneg[:, 0:1], op=ALU.pow),
                          reads=[sB_, buf("cneg")], writes=[sB_])
                    r, rB, rhB = rbuf(t)
                    tk.op(P_, lambda e: e.tensor_tensor(out=r[:, 0:512], in0=r[:, 0:512], in1=fng_t[:, 0:512], op=ALU.mult),
                          reads=[rB, buf("fng_t")], writes=[rB])
                    tk.op(V, lambda e: e.tensor_tensor(out=r[:, 512:1024], in0=r[:, 512:1024], in1=fng_t[:, 512:1024],
                                                       op=ALU.mult),
                          reads=[rhB, buf("fng_t")], writes=[rhB])

                def D_s2b(t):
                    r, rB, rhB = rbuf(t)
                    c = t % 4
                    sB_ = buf("ssD%d" % c)
                    o_s, o_sB = ost[t % 2], buf("ost%d" % (t % 2))
                    tk.op(S_, lambda e: e.activation(out=o_s[:, :], in_=r[:, :], func=AF.Copy, scale=ssD[:, c, 2:3]),
                          reads=[rB, rhB, sB_], writes=[o_sB])
                    tk.dma(Y_, out_d[b, t * 128:(t + 1) * 128, :], o_s[:, :], reads=[o_sB])

                for step in range(16 + 4):
                    if step < 16:
                        D_s0(step)
                    if 0 <= step - 4 < 16:
                        D_s2b(step - 4)
                    if 0 <= step - 2 < 16:
                        D_s1b(step - 2)
                    if 0 <= step - 1 < 16:
                        D_s1a(step - 1)
                    if 0 <= step - 3 < 16:
                        D_s2a(step - 3)
            barrier()

        for b_ in range(nb):
            do_batch(b_)

        for bb in list(B.values()):
            if bb.dsem is not None and bb.dcnt > 0:
                tk.wait_ev(Y_, (bb.dsem, bb.dcnt, None))
        tk.emit()
    return nc


def _prep_inputs(inputs, nb=NB, ncores=8):
    f = np.float32
    x = np.ascontiguousarray(inputs["x"], dtype=f)
    ctx = np.ascontiguousarray(inputs["ctx"], dtype=f)
    c = np.asarray(inputs["c"], dtype=f)
    c_ctx = np.asarray(inputs["c_ctx"], dtype=f)
    shared = {
        "norm_g": np.ascontiguousarray(inputs["norm_g"][0:1], dtype=f),
        "w_ada": np.ascontiguousarray(inputs["w_ada"][0], dtype=f),
        "b_ada": np.ascontiguousarray(inputs["b_ada"][0:1], dtype=f),
        "w_in": np.ascontiguousarray(inputs["w_in"][0], dtype=f),
        "w_out": np.ascontiguousarray(inputs["w_out"][0], dtype=f),
        "final_norm_g": np.ascontiguousarray(np.asarray(inputs["final_norm_g"], dtype=f)[None, :]),
        "rng_col": np.ascontiguousarray(np.asarray(inputs["ret_norm_g"][0], dtype=f).reshape(4, 128).T),
        "decay8": np.ascontiguousarray(np.concatenate([np.asarray(inputs["ret_decay_fwd"][0], dtype=f),
                                                       np.asarray(inputs["ret_decay_bwd"][0], dtype=f)])[None, :]),
        "bias_tab": _bias_table(np.asarray(inputs["na_rpb"][0], dtype=f)),
    }
    shared.update(_consts())
    maps = []
    for i in range(ncores):
        m = dict(shared)
        m["x"] = x[i * nb:(i + 1) * nb]
        m["ctx"] = ctx[i * nb:(i + 1) * nb]
        rows = [c[i * nb + j] for j in range(nb)]
        while len(rows) < 2:
            rows.append(rows[-1])
        m["c3"] = np.ascontiguousarray(np.stack(rows + [c_ctx], axis=0))
        maps.append(m)
    return maps


def kernel(**inputs):
    nc = build_nc()
    maps = _prep_inputs(inputs)
    res = run_bass_kernel_spmd(nc, maps, core_ids=list(range(8)))
    return np.concatenate([r["out"] for r in res.results], axis=0)
```
